# Optimizing a Trainium2 kernel written in Bass

```python
import jax, jax.numpy as jnp
from jax import lax
import numpy as np

D_MODEL = 2048
BATCH = 8
SEQ = 2048
DEPTH = 4

CHUNK = 64
N_MIXERS = 3
ATTN_HEADS = 32
ATTN_KV_HEADS = 4
ATTN_HEAD_DIM = D_MODEL // ATTN_HEADS
ATTN_GROUP = ATTN_HEADS // ATTN_KV_HEADS
WINDOW = 128
WINDOW_CHUNKS = WINDOW // CHUNK
ATTN_BLOCK = 128
CONV_WIDTH = 31
MLSTM_HEADS = 8
MLSTM_V_DIM = D_MODEL // MLSTM_HEADS
MLSTM_QK_DIM = MLSTM_V_DIM // 2
MLSTM_CHUNK = CHUNK
D_FF = 4 * D_MODEL
EPS = 1e-6
NEG = -1e30

kernel_name = "hybrid_swa_conformer_mlstm_trunk"


def rms_norm(x, g):
    xf = x.astype(jnp.float32)
    y = xf * lax.rsqrt(jnp.mean(jnp.square(xf), axis=-1, keepdims=True) + EPS)
    return (y * g.astype(jnp.float32)).astype(x.dtype)


def layer_norm(x, g, b):
    xf = x.astype(jnp.float32)
    mu = jnp.mean(xf, axis=-1, keepdims=True)
    var = jnp.mean(jnp.square(xf - mu), axis=-1, keepdims=True)
    y = (xf - mu) * lax.rsqrt(var + EPS) * g.astype(jnp.float32) + b.astype(jnp.float32)
    return y.astype(x.dtype)


def sliding_window_sink_attention(h, w_qkv, q_norm, k_norm, sinks, w_o):
    B, S, _ = h.shape
    H, KVH, G, Dh, QB = ATTN_HEADS, ATTN_KV_HEADS, ATTN_GROUP, ATTN_HEAD_DIM, ATTN_BLOCK
    nb = S // QB
    qkv = h @ w_qkv
    q, k, v = jnp.split(qkv, [H * Dh, H * Dh + KVH * Dh], axis=-1)
    q = rms_norm(q.reshape(B, S, KVH, G, Dh), q_norm)
    k = rms_norm(k.reshape(B, S, KVH, Dh), k_norm)
    v = v.reshape(B, S, KVH, Dh)
    qb = q.reshape(B, nb, QB, KVH, G, Dh)

    def band(t):
        tb = t.reshape(B, nb, QB, KVH, Dh)
        prev = jnp.pad(tb[:, :-1], ((0, 0), (1, 0), (0, 0), (0, 0), (0, 0)))
        return jnp.concatenate([prev, tb], axis=2)

    kb, vb = band(k), band(v)
    scores = jnp.einsum("bnqhgd,bnkhd->bnhgqk", qb, kb).astype(jnp.float32) * (Dh ** -0.5)
    blk = jnp.arange(nb)[:, None, None]
    q_pos = blk * QB + jnp.arange(QB)[None, :, None]
    k_pos = blk * QB - QB + jnp.arange(2 * QB)[None, None, :]
    q_chunk, k_chunk = q_pos // CHUNK, k_pos // CHUNK
    allowed = (k_pos >= 0) & (k_chunk <= q_chunk) & (k_chunk >= q_chunk - WINDOW_CHUNKS)
    scores = jnp.where(allowed[None, :, None, None, :, :], scores, NEG)
    s = sinks.astype(jnp.float32).reshape(KVH, G)[None, None, :, :, None, None]
    m = jnp.maximum(jnp.max(scores, axis=-1, keepdims=True), s)
    p = jnp.exp(scores - m)
    probs = p / (jnp.sum(p, axis=-1, keepdims=True) + jnp.exp(s - m))
    out = jnp.einsum("bnhgqk,bnkhd->bnqhgd", probs.astype(vb.dtype), vb)
    return out.reshape(B, S, H * Dh) @ w_o


def conformer_conv_module(h, w_in, b_in, dw, dw_b, ln_g, ln_b, w_out, b_out):
    u = h @ w_in + b_in
    a, gate = jnp.split(u, 2, axis=-1)
    u = a * jax.nn.sigmoid(gate)
    u = lax.conv_general_dilated(
        u, dw[:, None, :], window_strides=(1,), padding=[(CONV_WIDTH - 1, 0)],
        dimension_numbers=("NWC", "WIO", "NWC"), feature_group_count=D_MODEL) + dw_b
    u = jax.nn.silu(layer_norm(u, ln_g, ln_b))
    return u @ w_out + b_out


def mlstm_chunkwise(h, w_in, b_gates, h_norm, w_out):
    B, S, _ = h.shape
    H, DK, DV, L = MLSTM_HEADS, MLSTM_QK_DIM, MLSTM_V_DIM, MLSTM_CHUNK
    nc = S // L
    f32 = jnp.float32
    proj = h @ w_in
    q, k, v, o_pre, gates_pre = jnp.split(
        proj, [H * DK, 2 * H * DK, 2 * H * DK + D_MODEL, 2 * H * DK + 2 * D_MODEL], axis=-1)
    o_gate = jax.nn.sigmoid(o_pre + b_gates[:D_MODEL])
    gates = (gates_pre + b_gates[D_MODEL:]).astype(f32)
    i_pre, f_pre = gates[..., :H], gates[..., H:]
    logf = jax.nn.log_sigmoid(f_pre)

    def heads(t, d):
        return t.reshape(B, nc, L, H, d).transpose(0, 3, 1, 2, 4).astype(f32)

    def gate_layout(t):
        return t.reshape(B, nc, L, H).transpose(0, 3, 1, 2)

    q = heads(q, DK) * (DK ** -0.5)
    k, v = heads(k, DK), heads(v, DV)
    i_c, lf_c = gate_layout(i_pre), gate_layout(logf)
    b = jnp.cumsum(lf_c, axis=-1)
    g = b[..., -1]
    a = g[..., None] - b + i_c

    def step(carry, inp):
        C, n, m = carry
        k_c, v_c, a_c, g_c = inp
        m_new = jnp.maximum(g_c + m, jnp.max(a_c, axis=-1))
        decay = jnp.exp(g_c + m - m_new)
        kw = k_c * jnp.exp(a_c - m_new[..., None])[..., None]
        C_new = decay[..., None, None] * C + jnp.einsum("bhlk,bhlv->bhkv", kw, v_c)
        n_new = decay[..., None] * n + jnp.sum(kw, axis=-2)
        return (C_new, n_new, m_new), (C, n, m)

    init = (jnp.zeros((B, H, DK, DV), f32), jnp.zeros((B, H, DK), f32), jnp.zeros((B, H), f32))
    xs = (jnp.moveaxis(k, 2, 0), jnp.moveaxis(v, 2, 0), jnp.moveaxis(a, 2, 0), jnp.moveaxis(g, 2, 0))
    _, (C_prev, n_prev, m_prev) = lax.scan(step, init, xs)
    C_prev = jnp.moveaxis(C_prev, 0, 2)
    n_prev = jnp.moveaxis(n_prev, 0, 2)
    m_prev = jnp.moveaxis(m_prev, 0, 2)

    causal = jnp.tril(jnp.ones((L, L), dtype=bool))
    dmat = jnp.where(causal, b[..., :, None] - b[..., None, :] + i_c[..., None, :], NEG)
    inter_log = b + m_prev[..., None]
    m_row = jnp.maximum(inter_log, jnp.max(dmat, axis=-1))
    w_intra = jnp.exp(dmat - m_row[..., None])
    w_inter = jnp.exp(inter_log - m_row)
    qk = jnp.einsum("bhnld,bhnsd->bhnls", q, k) * w_intra
    num = jnp.einsum("bhnls,bhnsv->bhnlv", qk, v) + \
        w_inter[..., None] * jnp.einsum("bhnld,bhndv->bhnlv", q, C_prev)
    qn = jnp.sum(qk, axis=-1) + w_inter * jnp.einsum("bhnld,bhnd->bhnl", q, n_prev)
    h_tilde = num / jnp.maximum(jnp.abs(qn), jnp.exp(-m_row))[..., None]
    h_tilde = h_tilde.transpose(0, 2, 3, 1, 4).reshape(B, S, H, DV)
    h_tilde = rms_norm(h_tilde, h_norm.reshape(H, DV)).reshape(B, S, D_MODEL).astype(h.dtype)
    return (o_gate * h_tilde) @ w_out


def squared_relu_mlp(h, w1, w2):
    return jnp.square(jax.nn.relu(h @ w1)) @ w2


def _normal(key, shape, scale):
    return jax.random.normal(key, shape, jnp.float32) * scale


def _gain(key, shape):
    return 1.0 + 0.02 * jax.random.normal(key, shape, jnp.float32)


def _attn_params(key, p):
    ks = jax.random.split(key, 5)
    qkv_cols = (ATTN_HEADS + 2 * ATTN_KV_HEADS) * ATTN_HEAD_DIM
    return {
        p + "attn_w_qkv": _normal(ks[0], (D_MODEL, qkv_cols), D_MODEL ** -0.5),
        p + "attn_q_norm": _gain(ks[1], (ATTN_HEAD_DIM,)),
        p + "attn_k_norm": _gain(ks[2], (ATTN_HEAD_DIM,)),
        p + "attn_sinks": _normal(ks[3], (ATTN_HEADS,), 0.5),
        p + "attn_w_o": _normal(ks[4], (ATTN_HEADS * ATTN_HEAD_DIM, D_MODEL), (ATTN_HEADS * ATTN_HEAD_DIM) ** -0.5),
    }


def _conv_params(key, p):
    ks = jax.random.split(key, 8)
    return {
        p + "conv_w_in": _normal(ks[0], (D_MODEL, 2 * D_MODEL), D_MODEL ** -0.5),
        p + "conv_b_in": _normal(ks[1], (2 * D_MODEL,), 0.01),
        p + "conv_dw": _normal(ks[2], (CONV_WIDTH, D_MODEL), CONV_WIDTH ** -0.5),
        p + "conv_dw_b": _normal(ks[3], (D_MODEL,), 0.01),
        p + "conv_ln_g": _gain(ks[4], (D_MODEL,)),
        p + "conv_ln_b": _normal(ks[5], (D_MODEL,), 0.01),
        p + "conv_w_out": _normal(ks[6], (D_MODEL, D_MODEL), D_MODEL ** -0.5),
        p + "conv_b_out": _normal(ks[7], (D_MODEL,), 0.01),
    }


def _mlstm_params(key, p):
    ks = jax.random.split(key, 5)
    in_cols = 2 * MLSTM_HEADS * MLSTM_QK_DIM + 2 * D_MODEL + 2 * MLSTM_HEADS
    b_gates = jnp.concatenate([
        _normal(ks[1], (D_MODEL + MLSTM_HEADS,), 0.01),
        3.0 + _normal(ks[2], (MLSTM_HEADS,), 0.5),
    ])
    return {
        p + "mlstm_w_in": _normal(ks[0], (D_MODEL, in_cols), D_MODEL ** -0.5),
        p + "mlstm_b_gates": b_gates,
        p + "mlstm_h_norm": _gain(ks[3], (D_MODEL,)),
        p + "mlstm_w_out": _normal(ks[4], (D_MODEL, D_MODEL), D_MODEL ** -0.5),
    }


def setup_inputs(seed: int = 0) -> dict:
    key = jax.random.key(seed)
    keys = jax.random.split(key, DEPTH + 1)
    makers = (_attn_params, _conv_params, _mlstm_params)
    params = {"x": jax.random.normal(keys[0], (BATCH, SEQ, D_MODEL), jnp.float32)}
    for i in range(DEPTH):
        lk = jax.random.split(keys[i + 1], 5)
        p = f"l{i}_"
        params[p + "mix_norm"] = _gain(lk[0], (D_MODEL,))
        params.update(makers[i % N_MIXERS](lk[1], p))
        params[p + "mlp_norm"] = _gain(lk[2], (D_MODEL,))
        params[p + "mlp_w1"] = _normal(lk[3], (D_MODEL, D_FF), D_MODEL ** -0.5)
        params[p + "mlp_w2"] = _normal(lk[4], (D_FF, D_MODEL), D_FF ** -0.5)
    return params


def reference(x,
              l0_mix_norm, l0_attn_w_qkv, l0_attn_q_norm, l0_attn_k_norm, l0_attn_sinks, l0_attn_w_o,
              l0_mlp_norm, l0_mlp_w1, l0_mlp_w2,
              l1_mix_norm, l1_conv_w_in, l1_conv_b_in, l1_conv_dw, l1_conv_dw_b, l1_conv_ln_g, l1_conv_ln_b,
              l1_conv_w_out, l1_conv_b_out,
              l1_mlp_norm, l1_mlp_w1, l1_mlp_w2,
              l2_mix_norm, l2_mlstm_w_in, l2_mlstm_b_gates, l2_mlstm_h_norm, l2_mlstm_w_out,
              l2_mlp_norm, l2_mlp_w1, l2_mlp_w2,
              l3_mix_norm, l3_attn_w_qkv, l3_attn_q_norm, l3_attn_k_norm, l3_attn_sinks, l3_attn_w_o,
              l3_mlp_norm, l3_mlp_w1, l3_mlp_w2):
    mixer_fns = (sliding_window_sink_attention, conformer_conv_module, mlstm_chunkwise)
    mix_norms = (l0_mix_norm, l1_mix_norm, l2_mix_norm, l3_mix_norm)
    mixer_args = (
        (l0_attn_w_qkv, l0_attn_q_norm, l0_attn_k_norm, l0_attn_sinks, l0_attn_w_o),
        (l1_conv_w_in, l1_conv_b_in, l1_conv_dw, l1_conv_dw_b, l1_conv_ln_g, l1_conv_ln_b,
         l1_conv_w_out, l1_conv_b_out),
        (l2_mlstm_w_in, l2_mlstm_b_gates, l2_mlstm_h_norm, l2_mlstm_w_out),
        (l3_attn_w_qkv, l3_attn_q_norm, l3_attn_k_norm, l3_attn_sinks, l3_attn_w_o),
    )
    mlp_norms = (l0_mlp_norm, l1_mlp_norm, l2_mlp_norm, l3_mlp_norm)
    mlp_w1s = (l0_mlp_w1, l1_mlp_w1, l2_mlp_w1, l3_mlp_w1)
    mlp_w2s = (l0_mlp_w2, l1_mlp_w2, l2_mlp_w2, l3_mlp_w2)
    for i in range(DEPTH):
        x = x + mixer_fns[i % N_MIXERS](rms_norm(x, mix_norms[i]), *mixer_args[i])
        x = x + squared_relu_mlp(rms_norm(x, mlp_norms[i]), mlp_w1s[i], mlp_w2s[i])
    return x
```

```python
import numpy as np
import concourse.bass as bass
import concourse.mybir as mybir
from concourse.bass_utils import run_bass_kernel_spmd

F32 = mybir.dt.float32
BF16 = mybir.dt.bfloat16
AF = mybir.ActivationFunctionType
ALU = mybir.AluOpType

SAME_ENG_SYNC = True


class Op:
    __slots__ = ("eng", "fn", "eidx", "signal", "seq", "waits", "dma_key", "dma_cnt", "name")

    def __init__(self, eng, fn, name=""):
        self.eng = eng
        self.fn = fn
        self.eidx = -1
        self.signal = False
        self.seq = -1
        self.waits = []
        self.dma_key = None
        self.dma_cnt = 0
        self.name = name


class Sched:
    ENGS = ("pe", "act", "dve", "pool", "sp")

    def __init__(self, nc):
        self.nc = nc
        self.streams = {e: [] for e in self.ENGS}
        self.last_w = {}
        self.readers = {}
        self.waited = {e: {} for e in self.ENGS}
        self.dma_cnt = {}
        self.dma_last = {}
        self.dma_keys = []
        self.fence_ops = []

    def fence(self):
        f = []
        for e in self.ENGS:
            for o in reversed(self.streams[e]):
                if o.dma_key is None:
                    f.append(o)
                    break
        f.extend(self.dma_last.values())
        self.fence_ops = f

    def _stream_id(self, op):
        return ("dma", op.dma_key) if op.dma_key is not None else op.eng

    def _prog(self, op):
        return op.dma_cnt if op.dma_key is not None else op.eidx

    def op(self, eng, fn, reads=(), writes=(), dma_key=None, name=""):
        o = Op(eng, fn, name)
        o.eidx = len(self.streams[eng])
        deps = {}

        def add(d):
            if d is None:
                return
            sid = self._stream_id(d)
            if sid == eng and d.dma_key is None:
                if eng == "pe" or not SAME_ENG_SYNC:
                    return
            cur = deps.get(sid)
            if cur is None or self._prog(d) > self._prog(cur):
                deps[sid] = d

        if dma_key is not None:
            o.dma_key = dma_key
            if dma_key not in self.dma_cnt:
                self.dma_cnt[dma_key] = 0
                self.dma_keys.append(dma_key)
            add(self.dma_last.get(dma_key))
            self.dma_cnt[dma_key] += 1
            o.dma_cnt = self.dma_cnt[dma_key]
            self.dma_last[dma_key] = o
        for k in reads:
            add(self.last_w.get(k))
        for k in writes:
            add(self.last_w.get(k))
            for r in self.readers.get(k, {}).values():
                add(r)
        for d in self.fence_ops:
            add(d)
        wd = self.waited[eng]
        for sid, d in deps.items():
            p = self._prog(d)
            if wd.get(sid, -1) >= p:
                continue
            wd[sid] = p
            o.waits.append(d)
            if d.dma_key is None:
                d.signal = True
        for k in writes:
            self.last_w[k] = o
            self.readers[k] = {}
        sid = self._stream_id(o)
        for k in reads:
            self.readers.setdefault(k, {})[sid] = o
        self.streams[eng].append(o)
        return o

    def I(self, eng, meth, *args, reads=(), writes=(), dma_key=None, **kw):
        return self.op(eng, lambda e: getattr(e, meth)(*args, **kw), reads, writes, dma_key)

    def emit(self):
        nc = self.nc
        for e in self.ENGS:
            c = 0
            for o in self.streams[e]:
                if o.dma_key is None and o.signal:
                    c += 1
                    o.seq = c
        from contextlib import ExitStack
        with ExitStack() as st:
            esem = {e: st.enter_context(nc.semaphore("s_" + e)) for e in self.ENGS}
            dsem = {k: st.enter_context(nc.semaphore("d_%d" % i)) for i, k in enumerate(self.dma_keys)}
            block = st.enter_context(nc.Block())

            def run(ename, eng):
                for o in self.streams[ename]:
                    for d in o.waits:
                        if d.dma_key is not None:
                            eng.wait_ge(dsem[d.dma_key], 16 * d.dma_cnt)
                        else:
                            eng.wait_ge(esem[d.eng], d.seq)
                    ins = o.fn(eng)
                    if o.dma_key is not None:
                        ins.then_inc(dsem[o.dma_key], 16)
                    elif o.signal:
                        ins.then_inc(esem[ename], 1)
                if ename == "sp":
                    for k in self.dma_keys:
                        eng.wait_ge(dsem[k], 16 * self.dma_cnt[k])

            @block.tensor
            def _(e):
                run("pe", e)

            @block.scalar
            def _(e):
                run("act", e)

            @block.vector
            def _(e):
                run("dve", e)

            @block.gpsimd
            def _(e):
                run("pool", e)

            @block.sync
            def _(e):
                run("sp", e)


D = 2048
SEQ = 2048
DFF = 8192
NDC = D // 128
NFC = DFF // 128
EPS = 1e-6
TH = 1024
NTT = TH // 512


class Ctx:
    def __init__(self, nc, S, st):
        self.nc, self.S, self.st = nc, S, st
        self.banks = [st.enter_context(nc.psum_tensor("psb%d" % i, [128, 512], F32)) for i in range(8)]
        self.bank_i = 0
        self.ones = st.enter_context(nc.sbuf_tensor("ones_bf", [128, 128], BF16))
        S.op("dve", lambda e: e.memset(self.ones[:], 1.0), writes=["ones"])
        self.eps_col = st.enter_context(nc.sbuf_tensor("eps_col", [128, 1], F32))
        S.op("dve", lambda e: e.memset(self.eps_col[:], EPS), writes=["consts"])
        self.uid = 0
        self.dbg = None

    def bank(self):
        b = self.bank_i
        self.bank_i = (b + 1) % 8
        return b, self.banks[b]

    def name(self, s):
        self.uid += 1
        return "%s_%d" % (s, self.uid)


def rms_stats(C, src, src_key, nch, inv_n, sq_ring, rstd_tile, rstd_key, tt, tmp_tile):
    S = C.S
    b, ps = C.bank()
    for dc in range(nch):
        r = dc % len(sq_ring)
        sq = sq_ring[r]
        S.op("act", lambda e, sq=sq, dc=dc: e.activation(out=sq[:], in_=src(dc), func=AF.Square),
             reads=[src_key(dc)], writes=[("sq", r)])
        S.op("pe", lambda e, sq=sq, dc=dc, ps=ps: e.matmul(ps[:], lhsT=C.ones[:], rhs=sq[:],
                                                          start=(dc == 0), stop=(dc == nch - 1)),
             reads=[("sq", r), "ones"], writes=[("ps", b)])
    S.op("act", lambda e, ps=ps: e.activation(out=tmp_tile[:], in_=ps[:], func=AF.Sqrt, scale=inv_n, bias=C.eps_col[:]),
         reads=[("ps", b), "consts"], writes=["rms_tmp"])
    S.op("dve", lambda e: e.reciprocal(out=rstd_tile, in_=tmp_tile[:]), reads=["rms_tmp"], writes=[rstd_key])


def mlp_phase(C, src, dst, gcol, w1, w2):
    nc, S = C.nc, C.S
    from contextlib import ExitStack
    G = 8
    R1, R2 = 3, 10
    with ExitStack() as st:
        acc = st.enter_context(nc.sbuf_tensor(C.name("acc"), [128, NDC, TH], F32))
        hT = st.enter_context(nc.sbuf_tensor(C.name("hT"), [128, NDC, TH], BF16))
        hid = st.enter_context(nc.sbuf_tensor(C.name("hid"), [128, G, TH], BF16))
        w1b = [st.enter_context(nc.sbuf_tensor(C.name("w1b"), [128, NDC, 256], BF16)) for _ in range(R1)]
        w2b = [st.enter_context(nc.sbuf_tensor(C.name("w2b"), [128, D], BF16)) for _ in range(R2)]
        sqr = [st.enter_context(nc.sbuf_tensor(C.name("sq"), [128, 512], BF16)) for _ in range(2)]
        rl = [st.enter_context(nc.sbuf_tensor(C.name("rl"), [128, 512], F32)) for _ in range(3)]
        rstd = st.enter_context(nc.sbuf_tensor(C.name("rstd"), [128, TH], F32))
        tmp = st.enter_context(nc.sbuf_tensor(C.name("rtmp"), [128, 512], F32))
        srcv = src.rearrange("(dc p) t -> p dc t", p=128)
        dstv = dst.rearrange("(dc p) t -> p dc t", p=128)
        w1v = w1.rearrange("(kc p) f -> p kc f", p=128)
        n1 = 0
        n2 = 0
        rli = 0
        for half in range(SEQ // TH):
            t0 = half * TH
            for q in range(4):
                S.op("sp", lambda e, q=q, t0=t0: e.dma_start(out=acc[:, 4 * q:4 * q + 4, :],
                                                            in_=srcv[:, 4 * q:4 * q + 4, t0:t0 + TH]),
                     reads=[("dram", id(src), half)], writes=[("acc", dc) for dc in range(4 * q, 4 * q + 4)],
                     dma_key=("acc", q))
            def load_w1(j):
                nonlocal n1
                s = n1 % R1
                n1 += 1
                S.op("pool", lambda e, s=s, j=j: e.dma_start(out=w1b[s][:], in_=w1v[:, :, j * 256:(j + 1) * 256]),
                     writes=[("w1b", s)], dma_key=("w1b", s))
                return s

            def load_w2(f):
                nonlocal n2
                s = n2 % R2
                n2 += 1
                S.op("pool", lambda e, s=s, f=f: e.dma_start(out=w2b[s][:], in_=w2[f * 128:(f + 1) * 128, :]),
                     writes=[("w2b", s)], dma_key=("w2b", s))
                return s
            w1slots = {}
            w2slots = {}
            w1slots[0] = load_w1(0)
            w1slots[1] = load_w1(1)
            for tt in range(NTT):
                rms_stats(C, lambda dc, tt=tt: acc[:, dc, tt * 512:(tt + 1) * 512], lambda dc: ("acc", dc), NDC,
                          1.0 / D, sqr, rstd[:, tt * 512:(tt + 1) * 512], ("rstd", tt), tt, tmp)
                for dc in range(NDC):
                    S.op("dve", lambda e, dc=dc, tt=tt: e.scalar_tensor_tensor(
                        out=hT[:, dc, tt * 512:(tt + 1) * 512], in0=acc[:, dc, tt * 512:(tt + 1) * 512],
                        scalar=gcol[:, dc:dc + 1], in1=rstd[:, tt * 512:(tt + 1) * 512],
                        op0=ALU.mult, op1=ALU.mult),
                        reads=[("acc", dc), ("rstd", tt), "consts"], writes=[("hT", dc, tt)])
            for f in range(G):
                w2slots[f] = load_w2(f)
            for g in range(NFC // G):
                for fl in range(G):
                    f = g * G + fl
                    j = f // 2
                    if f % 2 == 0 and j + 2 < NFC // 2:
                        w1slots[j + 2] = load_w1(j + 2)
                    s1 = w1slots[j]
                    c0 = (f % 2) * 128
                    for tt in range(NTT):
                        b, ps = C.bank()
                        for kc in range(NDC):
                            S.op("pe", lambda e, ps=ps, s1=s1, c0=c0, kc=kc, tt=tt: e.matmul(
                                ps[:], lhsT=w1b[s1][:, kc, c0:c0 + 128], rhs=hT[:, kc, tt * 512:(tt + 1) * 512],
                                start=(kc == 0), stop=(kc == NDC - 1)),
                                reads=[("w1b", s1), ("hT", kc, tt)], writes=[("ps", b)])
                        r = rli % len(rl)
                        rli += 1
                        S.op("act", lambda e, ps=ps, r=r: e.activation(out=rl[r][:], in_=ps[:], func=AF.Relu),
                             reads=[("ps", b)], writes=[("rl", r)])
                        S.op("dve", lambda e, r=r, fl=fl, tt=tt: e.tensor_tensor(
                            out=hid[:, fl, tt * 512:(tt + 1) * 512], in0=rl[r][:], in1=rl[r][:], op=ALU.mult),
                            reads=[("rl", r)], writes=[("hid", fl, tt)])
                nxt = [(g + 1) * G + fl for fl in range(G)] if g + 1 < NFC // G else []
                pre = nxt[:R2 - G]
                for f in pre:
                    w2slots[f] = load_w2(f)
                rest = nxt[R2 - G:]
                for dc in range(NDC):
                    for tt in range(NTT):
                        b, ps = C.bank()
                        for fl in range(G):
                            s2 = w2slots[g * G + fl]
                            S.op("pe", lambda e, ps=ps, s2=s2, dc=dc, fl=fl, tt=tt: e.matmul(
                                ps[:], lhsT=w2b[s2][:, dc * 128:(dc + 1) * 128], rhs=hid[:, fl, tt * 512:(tt + 1) * 512],
                                start=(fl == 0), stop=(fl == G - 1)),
                                reads=[("w2b", s2), ("hid", fl, tt)], writes=[("ps", b)])
                        S.op("dve", lambda e, ps=ps, dc=dc, tt=tt: e.tensor_tensor(
                            out=acc[:, dc, tt * 512:(tt + 1) * 512], in0=acc[:, dc, tt * 512:(tt + 1) * 512],
                            in1=ps[:], op=ALU.add),
                            reads=[("ps", b), ("acc", dc)], writes=[("acc", dc)])
                for f in rest:
                    w2slots[f] = load_w2(f)
            for q in range(4):
                S.op("sp", lambda e, q=q, t0=t0: e.dma_start(out=dstv[:, 4 * q:4 * q + 4, t0:t0 + TH],
                                                            in_=acc[:, 4 * q:4 * q + 4, :]),
                     reads=[("acc", dc) for dc in range(4 * q, 4 * q + 4)], writes=[("dram", id(dst), half)],
                     dma_key=("accst", q))


def load_norm_half(C, src, half, big, hT, gcol, rstd, tmp, sqr, th=TH):
    nc, S = C.nc, C.S
    srcv = src.rearrange("(dc p) t -> p dc t", p=128)
    t0 = half * th
    for q in range(4):
        S.op("sp", lambda e, q=q: e.dma_start(out=big[:, 4 * q:4 * q + 4, :], in_=srcv[:, 4 * q:4 * q + 4, t0:t0 + th]),
             writes=[("acc", dc) for dc in range(4 * q, 4 * q + 4)], dma_key=("acc", q))
    for tt in range(th // 512):
        sl = slice(tt * 512, (tt + 1) * 512)
        rms_stats(C, lambda dc, sl=sl: big[:, dc, sl], lambda dc: ("acc", dc), NDC, 1.0 / D, sqr,
                  rstd[:, sl], ("rstd", tt), tt, tmp)
        for dc in range(NDC):
            S.op("dve", lambda e, dc=dc, sl=sl: e.scalar_tensor_tensor(
                out=hT[:, dc, sl], in0=big[:, dc, sl], scalar=gcol[:, dc:dc + 1], in1=rstd[:, sl],
                op0=ALU.mult, op1=ALU.mult),
                reads=[("acc", dc), ("rstd", tt), "consts"], writes=[("hT", dc, tt)])


def out_proj_residual(C, src, dst, half, actT, act_key, w_out, bias_col, th=TH):
    nc, S = C.nc, C.S
    from contextlib import ExitStack
    srcv = src.rearrange("(dc p) t -> p dc t", p=128)
    dstv = dst.rearrange("(dc p) t -> p dc t", p=128)
    wv = w_out.rearrange("(kc p) f -> p kc f", p=128)
    t0 = half * th
    with ExitStack() as st:
        wob = [st.enter_context(nc.sbuf_tensor(C.name("wob"), [128, NDC, 128], BF16)) for _ in range(3)]
        xres = [st.enter_context(nc.sbuf_tensor(C.name("xres"), [128, th], F32)) for _ in range(3)]
        ob = [st.enter_context(nc.sbuf_tensor(C.name("ob"), [128, th], F32)) for _ in range(2)]

        def ld(dc):
            s = dc % 3
            S.op("pool", lambda e: e.dma_start(out=wob[s][:], in_=wv[:, :, dc * 128:(dc + 1) * 128]),
                 writes=[("wob", s)], dma_key=("wob", s))
            S.op("sp", lambda e: e.dma_start(out=xres[dc % 3][:], in_=srcv[:, dc, t0:t0 + th]),
                 writes=[("xres", dc % 3)], dma_key=("xres", dc % 3))
        ld(0)
        ld(1)
        for dc in range(NDC):
            if dc + 2 < NDC:
                ld(dc + 2)
            s = dc % 3
            for tt in range(th // 512):
                sl = slice(tt * 512, (tt + 1) * 512)
                b, ps = C.bank()
                for kc in range(NDC):
                    S.op("pe", lambda e, ps=ps, kc=kc, sl=sl, s=s: e.matmul(ps[:], lhsT=wob[s][:, kc, :], rhs=actT[:, kc, sl],
                                                                      start=(kc == 0), stop=(kc == NDC - 1)),
                         reads=[("wob", s), act_key(kc, tt)], writes=[("ps", b)])
                if bias_col is not None:
                    S.op("dve", lambda e, ps=ps, sl=sl, dc=dc: e.scalar_tensor_tensor(
                        out=ob[dc % 2][:, sl], in0=ps[:], scalar=bias_col[:, dc:dc + 1], in1=xres[dc % 3][:, sl],
                        op0=ALU.add, op1=ALU.add),
                        reads=[("ps", b), ("xres", dc % 3), "consts"], writes=[("ob", dc % 2, tt)])
                else:
                    S.op("dve", lambda e, ps=ps, sl=sl, dc=dc: e.tensor_tensor(
                        out=ob[dc % 2][:, sl], in0=ps[:], in1=xres[dc % 3][:, sl], op=ALU.add),
                        reads=[("ps", b), ("xres", dc % 3)], writes=[("ob", dc % 2, tt)])
            S.op("sp", lambda e, dc=dc: e.dma_start(out=dstv[:, dc, t0:t0 + th], in_=ob[dc % 2][:]),
                 reads=[("ob", dc % 2, tt) for tt in range(th // 512)], dma_key=("obst", dc % 2))
        S.fence()


CONVW = 31


def conv_phase(C, src, dst, P, w_in, w_out):
    nc, S = C.nc, C.S
    from contextlib import ExitStack
    S.fence()
    with ExitStack() as st:
        big = st.enter_context(nc.sbuf_tensor(C.name("cbig"), [128, NDC, TH], F32))
        hT = st.enter_context(nc.sbuf_tensor(C.name("chT"), [128, NDC, TH], BF16))
        halo = st.enter_context(nc.sbuf_tensor(C.name("halo"), [128, NDC, CONVW - 1], F32))
        rstd = st.enter_context(nc.sbuf_tensor(C.name("crstd"), [128, TH], F32))
        tmp = st.enter_context(nc.sbuf_tensor(C.name("ctmp"), [128, 512], F32))
        sqr = [st.enter_context(nc.sbuf_tensor(C.name("csq"), [128, 512], BF16)) for _ in range(3)]
        win_v = w_in.rearrange("(kc p) f -> p kc f", p=128)
        for half in range(SEQ // TH):
            load_norm_half(C, src, half, big, hT, P["mix_norm"], rstd, tmp, sqr)
            with ExitStack() as st2:
                wib = [st2.enter_context(nc.sbuf_tensor(C.name("wib"), [128, NDC, 256], BF16)) for _ in range(3)]
                ub = [st2.enter_context(nc.sbuf_tensor(C.name("ub"), [128, CONVW - 1 + TH], F32)) for _ in range(2)]
                sg = [st2.enter_context(nc.sbuf_tensor(C.name("sg"), [128, 512], F32)) for _ in range(2)]

                def ldw(c):
                    s = c % 3
                    S.op("pool", lambda e: e.dma_start(out=wib[s][:, :, 0:128], in_=win_v[:, :, c * 128:(c + 1) * 128]),
                         writes=[("wib", s, 0)], dma_key=("wib", s, 0))
                    S.op("pool", lambda e: e.dma_start(out=wib[s][:, :, 128:256],
                                                      in_=win_v[:, :, D + c * 128:D + (c + 1) * 128]),
                         writes=[("wib", s, 1)], dma_key=("wib", s, 1))
                ldw(0)
                ldw(1)
                sgi = 0
                for c in range(NDC):
                    if c + 2 < NDC:
                        ldw(c + 2)
                    s = c % 3
                    u = ub[c % 2]
                    uk = ("ub", c % 2)
                    if half == 0:
                        S.op("pool", lambda e, u=u: e.memset(u[:, 0:CONVW - 1], 0.0), writes=[uk])
                    else:
                        S.op("pool", lambda e, u=u, c=c: e.tensor_copy(out=u[:, 0:CONVW - 1], in_=halo[:, c, :]),
                             reads=[("halo", c)], writes=[uk])
                    for tt in range(NTT):
                        sl = slice(tt * 512, (tt + 1) * 512)
                        ba, psa = C.bank()
                        for kc in range(NDC):
                            S.op("pe", lambda e, psa=psa, kc=kc, sl=sl, s=s: e.matmul(
                                psa[:], lhsT=wib[s][:, kc, 0:128], rhs=hT[:, kc, sl], start=(kc == 0), stop=(kc == NDC - 1)),
                                reads=[("wib", s, 0), ("hT", kc, tt)], writes=[("ps", ba)])
                        bg, psg = C.bank()
                        for kc in range(NDC):
                            S.op("pe", lambda e, psg=psg, kc=kc, sl=sl, s=s: e.matmul(
                                psg[:], lhsT=wib[s][:, kc, 128:256], rhs=hT[:, kc, sl], start=(kc == 0), stop=(kc == NDC - 1)),
                                reads=[("wib", s, 1), ("hT", kc, tt)], writes=[("ps", bg)])
                        sgt = sg[sgi % 2]
                        sgk = ("sg", sgi % 2)
                        sgi += 1
                        S.op("act", lambda e, sgt=sgt, psg=psg, c=c: e.activation(
                            out=sgt[:], in_=psg[:], func=AF.Sigmoid, bias=P["b_in_g"][:, c:c + 1]),
                            reads=[("ps", bg), "consts"], writes=[sgk])
                        S.op("dve", lambda e, sgt=sgt, psa=psa, c=c, u=u, tt=tt: e.scalar_tensor_tensor(
                            out=u[:, CONVW - 1 + tt * 512:CONVW - 1 + (tt + 1) * 512], in0=psa[:],
                            scalar=P["b_in_a"][:, c:c + 1], in1=sgt[:], op0=ALU.add, op1=ALU.mult),
                            reads=[("ps", ba), sgk, "consts", uk], writes=[uk])
                    S.op("pool", lambda e, u=u, c=c: e.tensor_copy(out=halo[:, c, :], in_=u[:, TH:TH + CONVW - 1]),
                         reads=[uk], writes=[("halo", c)])
                    dwc = P["dw"]
                    S.op("dve", lambda e, u=u, c=c: e.tensor_scalar(
                        out=big[:, c, :], in0=u[:, 0:TH], scalar1=dwc[:, c * CONVW:c * CONVW + 1],
                        scalar2=P["dw_b"][:, c:c + 1], op0=ALU.mult, op1=ALU.add),
                        reads=[uk, "consts"], writes=[("acc", c)])
                    for j in range(1, CONVW):
                        S.op("dve", lambda e, u=u, c=c, j=j: e.scalar_tensor_tensor(
                            out=big[:, c, :], in0=u[:, j:j + TH], scalar=dwc[:, c * CONVW + j:c * CONVW + j + 1],
                            in1=big[:, c, :], op0=ALU.mult, op1=ALU.add),
                            reads=[uk, "consts", ("acc", c)], writes=[("acc", c)])
                S.fence()
            with ExitStack() as st2:
                vb = [st2.enter_context(nc.sbuf_tensor(C.name("vb"), [128, 512], BF16)) for _ in range(3)]
                vq = [st2.enter_context(nc.sbuf_tensor(C.name("vq"), [128, 512], BF16)) for _ in range(3)]
                mu = st2.enter_context(nc.sbuf_tensor(C.name("mu"), [128, 512], F32))
                musq = st2.enter_context(nc.sbuf_tensor(C.name("musq"), [128, 512], F32))
                var = st2.enter_context(nc.sbuf_tensor(C.name("var"), [128, 512], F32))
                rs = st2.enter_context(nc.sbuf_tensor(C.name("lrs"), [128, 512], F32))
                t1 = [st2.enter_context(nc.sbuf_tensor(C.name("t1"), [128, 512], F32)) for _ in range(2)]
                t2 = [st2.enter_context(nc.sbuf_tensor(C.name("t2"), [128, 512], F32)) for _ in range(2)]
                for tt in range(NTT):
                    sl = slice(tt * 512, (tt + 1) * 512)
                    b1, ps1 = C.bank()
                    b2, ps2 = C.bank()
                    for c in range(NDC):
                        r = c % 3
                        S.op("act", lambda e, r=r, c=c, sl=sl: e.activation(out=vb[r][:], in_=big[:, c, sl], func=AF.Copy),
                             reads=[("acc", c)], writes=[("vb", r)])
                        S.op("pe", lambda e, r=r, c=c, ps1=ps1: e.matmul(ps1[:], lhsT=C.ones[:], rhs=vb[r][:],
                                                                        start=(c == 0), stop=(c == NDC - 1)),
                             reads=[("vb", r), "ones"], writes=[("ps", b1)])
                        S.op("act", lambda e, r=r, c=c, sl=sl: e.activation(out=vq[r][:], in_=big[:, c, sl], func=AF.Square),
                             reads=[("acc", c)], writes=[("vq", r)])
                        S.op("pe", lambda e, r=r, c=c, ps2=ps2: e.matmul(ps2[:], lhsT=C.ones[:], rhs=vq[r][:],
                                                                        start=(c == 0), stop=(c == NDC - 1)),
                             reads=[("vq", r), "ones"], writes=[("ps", b2)])
                    S.op("act", lambda e, ps1=ps1: e.activation(out=mu[:], in_=ps1[:], func=AF.Copy, scale=1.0 / D),
                         reads=[("ps", b1)], writes=["mu"])
                    S.op("act", lambda e, ps1=ps1: e.activation(out=musq[:], in_=ps1[:], func=AF.Square, scale=1.0 / D),
                         reads=[("ps", b1)], writes=["musq"])
                    S.op("dve", lambda e, ps2=ps2: e.scalar_tensor_tensor(
                        out=var[:], in0=ps2[:], scalar=1.0 / D, in1=musq[:], op0=ALU.mult, op1=ALU.subtract),
                        reads=[("ps", b2), "musq"], writes=["var"])
                    S.op("act", lambda e: e.activation(out=musq[:], in_=var[:], func=AF.Sqrt, bias=C.eps_col[:]),
                         reads=["var", "consts"], writes=["musq"])
                    S.op("dve", lambda e: e.reciprocal(out=rs[:], in_=musq[:]), reads=["musq"], writes=["lrs"])
                    for c in range(NDC):
                        r = c % 2
                        S.op("pool", lambda e, r=r, c=c, sl=sl: e.tensor_tensor(out=t1[r][:], in0=big[:, c, sl], in1=mu[:],
                                                                               op=ALU.subtract),
                             reads=[("acc", c), "mu"], writes=[("t1", r)])
                        S.op("dve", lambda e, r=r: e.tensor_tensor(out=t2[r][:], in0=t1[r][:], in1=rs[:], op=ALU.mult),
                             reads=[("t1", r), "lrs"], writes=[("t2", r)])
                        S.op("act", lambda e, r=r, c=c, sl=sl: e.activation(
                            out=hT[:, c, sl], in_=t2[r][:], func=AF.Silu, scale=P["ln_g"][:, c:c + 1],
                            bias=P["ln_b"][:, c:c + 1]),
                            reads=[("t2", r), "consts"], writes=[("hT", c, tt)])
                S.fence()
            if C.dbg is not None and half == 0:
                S.op("sp", lambda e: e.dma_start(out=C.dbg["v"], in_=big[:]), reads=[("acc", c) for c in range(NDC)], dma_key="dbgv")
                S.op("sp", lambda e: e.dma_start(out=C.dbg["s"], in_=hT[:]), reads=[("hT", c, tt) for c in range(NDC) for tt in range(NTT)], dma_key="dbgs")
            out_proj_residual(C, src, dst, half, hT, lambda kc, tt: ("hT", kc, tt), w_out, P["b_out"])
    S.fence()


NEGBIG = -30000.0


def attn_consts(C):
    nc, S, st = C.nc, C.S, C.st
    C.bdones = st.enter_context(nc.sbuf_tensor("bdones", [128, 128], BF16))
    C.amask = st.enter_context(nc.sbuf_tensor("amask", [128, 128], BF16))
    C.bmask = st.enter_context(nc.sbuf_tensor("bmask", [128, 256], BF16))
    S.I("pool", "memset", C.bdones[:], 0.0, writes=["bdones"])
    S.I("pool", "memset", C.bdones[0:64, 0:64], 1.0, reads=["bdones"], writes=["bdones"])
    S.I("pool", "memset", C.bdones[64:128, 64:128], 1.0, reads=["bdones"], writes=["bdones"])
    S.I("pool", "memset", C.amask[:], 0.0, writes=["amask"])
    S.I("pool", "memset", C.bmask[:], 0.0, writes=["bmask"])
    for base in (0, 64):
        S.I("pool", "memset", C.amask[base:base + 1, 64:128], 1.0, reads=["amask"], writes=["amask"])
        S.I("pool", "memset", C.amask[base + 32:base + 33, 0:64], 1.0, reads=["amask"], writes=["amask"])
        S.I("pool", "memset", C.bmask[base:base + 1, 0:64], NEGBIG, reads=["bmask"], writes=["bmask"])
        S.I("pool", "memset", C.bmask[base + 32:base + 33, 192:256], NEGBIG, reads=["bmask"], writes=["bmask"])


def head_rms_evac(C, ps, b, gcol, out_ap, out_key, sq, sqk, tmp, tmpk, rs, rsk, inv_n, ones_lhsT):
    S = C.S
    S.I("act", "activation", out=sq[:], in_=ps[:], func=AF.Square, reads=[("ps", b)], writes=[sqk])
    b2, ps2 = C.bank()
    S.I("pe", "matmul", ps2[:], lhsT=ones_lhsT, rhs=sq[:], start=True, stop=True,
        reads=[sqk, "bdones", "ones"], writes=[("ps", b2)])
    S.I("act", "activation", out=tmp[:], in_=ps2[:], func=AF.Sqrt, scale=inv_n, bias=C.eps_col[:],
        reads=[("ps", b2), "consts"], writes=[tmpk])
    S.I("dve", "reciprocal", out=rs[:], in_=tmp[:], reads=[tmpk], writes=[rsk])
    S.I("dve", "scalar_tensor_tensor", out=out_ap, in0=ps[:], scalar=gcol, in1=rs[:], op0=ALU.mult, op1=ALU.mult,
        reads=[("ps", b), rsk, "consts"], writes=[out_key])


def attn_phase(C, src, dst, P, w_qkv, w_o):
    nc, S = C.nc, C.S
    from contextlib import ExitStack
    S.fence()
    NB = TH // 128
    with ExitStack() as st:
        hT = st.enter_context(nc.sbuf_tensor(C.name("ahT"), [128, NDC, TH], BF16))
        qT = st.enter_context(nc.sbuf_tensor(C.name("aqT"), [128, NDC, TH], BF16))
        kT = [st.enter_context(nc.sbuf_tensor(C.name("akT"), [128, 128 + TH], BF16)) for _ in range(4)]
        V = st.enter_context(nc.sbuf_tensor(C.name("aV"), [128, NB + 1, 256], BF16))
        esink = st.enter_context(nc.sbuf_tensor(C.name("esink"), [128, NDC], F32))
        S.I("act", "activation", out=esink[:], in_=P["sinks"], func=AF.Exp, reads=["consts"], writes=["esink"])
        wv_ = w_qkv.rearrange("(kc p) f -> p kc f", p=128)
        for half in range(SEQ // TH):
            with ExitStack() as st2:
                big = st2.enter_context(nc.sbuf_tensor(C.name("abig"), [128, NDC, TH], F32))
                rstd = st2.enter_context(nc.sbuf_tensor(C.name("arstd"), [128, TH], F32))
                tmp = st2.enter_context(nc.sbuf_tensor(C.name("atmp"), [128, 512], F32))
                sqr = [st2.enter_context(nc.sbuf_tensor(C.name("asq"), [128, 512], BF16)) for _ in range(3)]
                load_norm_half(C, src, half, big, hT, P["mix_norm"], rstd, tmp, sqr)
                S.fence()
            with ExitStack() as st2:
                wq = [st2.enter_context(nc.sbuf_tensor(C.name("wq"), [128, NDC, 128], BF16)) for _ in range(3)]
                wk = [st2.enter_context(nc.sbuf_tensor(C.name("wk"), [128, NDC, 128], BF16)) for _ in range(2)]
                wvt = st2.enter_context(nc.sbuf_tensor(C.name("wvt"), [128, NDC, 256], BF16))
                sq = [st2.enter_context(nc.sbuf_tensor(C.name("bsq"), [128, 512], BF16)) for _ in range(2)]
                tm = [st2.enter_context(nc.sbuf_tensor(C.name("btm"), [128, 512], F32)) for _ in range(2)]
                rs = [st2.enter_context(nc.sbuf_tensor(C.name("brs"), [128, 512], F32)) for _ in range(2)]
                ei = 0

                def ldq(c):
                    s = c % 3
                    S.I("pool", "dma_start", out=wq[s][:], in_=wv_[:, :, c * 128:(c + 1) * 128],
                        writes=[("wq", s)], dma_key=("wq", s))

                def ldk(g):
                    s = g % 2
                    for d2 in range(2):
                        S.I("pool", "dma_start", out=wk[s][:, :, d2 * 64:(d2 + 1) * 64],
                            in_=wv_[:, :, D + g * 64:D + (g + 1) * 64], writes=[("wk", s, d2)], dma_key=("wk", s, d2))
                S.I("pool", "dma_start", out=wvt[:], in_=wv_[:, :, D + 256:D + 512], writes=["wvt"], dma_key="wvt")
                ldk(0)
                ldk(1)
                ldq(0)
                ldq(1)
                if half > 0:
                    for g in range(4):
                        S.I("pool", "tensor_copy", out=kT[g][:, 0:128], in_=kT[g][:, TH:TH + 128],
                            reads=[("kT", g, "w1")], writes=[("kT", g, 0)])
                    S.I("pool", "tensor_copy", out=V[:, 0, :], in_=V[:, NB, :], reads=[("V", NB)], writes=[("V", 0)])
                for tb in range(NB):
                    b, ps = C.bank()
                    for kc in range(NDC):
                        S.I("pe", "matmul", ps[:, 0:256], lhsT=hT[:, kc, tb * 128:(tb + 1) * 128], rhs=wvt[:, kc, :],
                            start=(kc == 0), stop=(kc == NDC - 1), reads=[("hT", kc, tb // 4), "wvt"], writes=[("ps", b)])
                    S.I("act", "activation", out=V[:, tb + 1, :], in_=ps[:, 0:256], func=AF.Copy,
                        reads=[("ps", b)], writes=[("V", tb + 1)])
                for g in range(4):
                    if g + 2 < 4:
                        pass
                    s = g % 2
                    for tt in range(NTT):
                        b, ps = C.bank()
                        for kc in range(NDC):
                            S.I("pe", "matmul", ps[:], lhsT=wk[s][:, kc, :], rhs=hT[:, kc, tt * 512:(tt + 1) * 512],
                                start=(kc == 0), stop=(kc == NDC - 1),
                                reads=[("wk", s, 0), ("wk", s, 1), ("hT", kc, tt)], writes=[("ps", b)])
                        e2 = ei % 2
                        ei += 1
                        head_rms_evac(C, ps, b, P["gk"], kT[g][:, 128 + tt * 512:128 + (tt + 1) * 512],
                                      ("kT", g, "w%d" % tt), sq[e2], ("bsq", e2), tm[e2], ("btm", e2), rs[e2], ("brs", e2),
                                      1.0 / 64, C.bdones[:])
                    if g + 2 < 4:
                        ldk(g + 2)
                for c in range(NDC):
                    if c + 2 < NDC:
                        ldq(c + 2)
                    s = c % 3
                    for tt in range(NTT):
                        b, ps = C.bank()
                        for kc in range(NDC):
                            S.I("pe", "matmul", ps[:], lhsT=wq[s][:, kc, :], rhs=hT[:, kc, tt * 512:(tt + 1) * 512],
                                start=(kc == 0), stop=(kc == NDC - 1), reads=[("wq", s), ("hT", kc, tt)], writes=[("ps", b)])
                        e2 = ei % 2
                        ei += 1
                        head_rms_evac(C, ps, b, P["gq"], qT[:, c, tt * 512:(tt + 1) * 512], ("qT", c, tt),
                                      sq[e2], ("bsq", e2), tm[e2], ("btm", e2), rs[e2], ("brs", e2), 1.0 / 64, C.bdones[:])
                S.fence()
            with ExitStack() as st2:
                NPT = 8
                PT = [st2.enter_context(nc.sbuf_tensor(C.name("PT"), [128, 256], BF16)) for _ in range(NPT)]
                dn = [st2.enter_context(nc.sbuf_tensor(C.name("dn"), [128, 512], F32)) for _ in range(2)]
                rc = [st2.enter_context(nc.sbuf_tensor(C.name("rc"), [128, 512], F32)) for _ in range(2)]
                pti = 0
                nrm = 0
                sbanks = [0, 1, 2, 3]
                sbi = 0
                nbanks = [(4, 5), (6, 7)]
                nbi = 0
                for c in range(NDC):
                    g = c // 4
                    prevPT = {0: None, 64: None}
                    for j in range(-1, NB):
                        if j == -1 and half == 0:
                            continue
                        kb = j + 1
                        if j == -1:
                            q0, n, m0 = 0, 128, 128
                        elif j == NB - 1:
                            q0, n, m0 = j * 128, 128, 0
                        else:
                            q0, n, m0 = j * 128, 256, 0
                        curPT = {}
                        for base in (0, 64):
                            b = sbanks[sbi % 4]
                            sbi += 1
                            ps = C.banks[b]
                            tts = sorted({q0 // 512, (q0 + n - 1) // 512})
                            S.I("pe", "matmul", ps[:, 0:n], lhsT=kT[g][base:base + 64, kb * 128:(kb + 1) * 128],
                                rhs=qT[base:base + 64, c, q0:q0 + n], start=True, stop=False,
                                reads=[("kT", g, 0), ("kT", g, "w0"), ("kT", g, "w1")] + [("qT", c, t) for t in tts],
                                writes=[("ps", b)])
                            S.I("pe", "matmul", ps[:, 0:n], lhsT=C.amask[base:base + 64, :],
                                rhs=C.bmask[base:base + 64, m0:m0 + n], start=False, stop=True,
                                reads=["amask", "bmask"], writes=[("ps", b)])
                            pt = pti % NPT
                            pti += 1
                            S.I("act", "activation", out=PT[pt][:, 0:n], in_=ps[:, 0:n], func=AF.Exp, scale=0.125,
                                reads=[("ps", b)], writes=[("PT", pt)])
                            curPT[base] = (pt, n, m0)
                        if j >= 0:
                            if j % 4 == 0:
                                bn, bd = nbanks[nbi % 2]
                                nbi += 1
                            cols = slice((j % 4) * 128, (j % 4 + 1) * 128)
                            for (bk, lw) in ((bn, None), (bd, C.ones[:, 0:64])):
                                for base in (0, 64):
                                    have_prev = prevPT[base] is not None
                                    if have_prev:
                                        ppt, pn, pm0 = prevPT[base]
                                        pc = slice(128, 256) if pn == 256 else slice(0, 128)
                                        S.I("pe", "matmul", C.banks[bk][base:base + 64, cols],
                                            lhsT=(V[:, kb - 1, g * 64:(g + 1) * 64] if lw is None else lw),
                                            rhs=PT[ppt][:, pc], start=True, stop=False,
                                            reads=[("V", kb - 1), ("PT", ppt), "ones"], writes=[("ps", bk)])
                                    cpt = curPT[base][0]
                                    S.I("pe", "matmul", C.banks[bk][base:base + 64, cols],
                                        lhsT=(V[:, kb, g * 64:(g + 1) * 64] if lw is None else lw),
                                        rhs=PT[cpt][:, 0:128], start=(not have_prev), stop=True,
                                        reads=[("V", kb), ("PT", cpt), "ones"], writes=[("ps", bk)])
                            if j % 4 == 3:
                                r = nrm % 2
                                nrm += 1
                                ocols = slice((j - 3) * 128, (j + 1) * 128)
                                S.I("dve", "tensor_scalar", out=dn[r][:], in0=C.banks[bd][:], scalar1=esink[:, c:c + 1],
                                    scalar2=None, op0=ALU.add, reads=[("ps", bd), "esink"], writes=[("dn", r)])
                                S.I("dve", "reciprocal", out=rc[r][:], in_=dn[r][:], reads=[("dn", r)], writes=[("rc", r)])
                                S.I("dve", "tensor_tensor", out=hT[:, c, ocols], in0=C.banks[bn][:], in1=rc[r][:], op=ALU.mult,
                                    reads=[("ps", bn), ("rc", r)], writes=[("hT", c, j // 4)])
                        prevPT = curPT
                S.fence()
            out_proj_residual(C, src, dst, half, hT, lambda kc, tt: ("hT", kc, tt), w_o, None)
    S.fence()


MTH = 512
MLSTM_PASSES = 2048 // 512
MLSTM_DBG = ''


class _Stop(Exception):
    pass


def _chk(tag):
    return MLSTM_DBG == tag
MH = 8
DK = 128
DV = 256


def mlstm_consts(C):
    nc, S, st = C.nc, C.S, C.st
    C.one_col = st.enter_context(nc.sbuf_tensor("one_col", [128, 1], F32))
    S.I("pool", "memset", C.one_col[:], 1.0, writes=["consts"])
    C.ones8 = st.enter_context(nc.sbuf_tensor("ones8", [8, 1024], F32))
    S.I("pool", "memset", C.ones8[:], 1.0, writes=["ones8"])
    C.ident = st.enter_context(nc.sbuf_tensor("ident", [128, 128], BF16))
    C.cmask = st.enter_context(nc.sbuf_tensor("cmask", [128, 128], BF16))
    C.sel = st.enter_context(nc.sbuf_tensor("sel", [128, MH, 128], BF16))
    S.I("pool", "affine_select", out=C.ident[:], in_=C.ones[:], pattern=[[1, 128]], compare_op=ALU.is_equal, fill=0.0,
        base=0, channel_multiplier=-1, reads=["ones"], writes=["ident"])
    S.I("pool", "affine_select", out=C.cmask[:], in_=C.ones[:], pattern=[[1, 128]], compare_op=ALU.is_ge, fill=0.0,
        base=0, channel_multiplier=-1, reads=["ones"], writes=["cmask"])
    S.I("pool", "memset", C.cmask[0:64, 64:128], 0.0, reads=["cmask"], writes=["cmask"])
    S.I("pool", "memset", C.sel[:], 0.0, writes=["sel"])
    S.I("pool", "affine_select", out=C.sel[0:8, :, :], in_=C.ones8[:, 0:MH * 128].rearrange("p (h m) -> p h m", m=128),
        pattern=[[1, MH], [0, 128]], compare_op=ALU.is_equal, fill=0.0, base=0, channel_multiplier=-1,
        reads=["ones8", "sel"], writes=["sel"])


def mlstm_phase(C, src, dst, P, w_in, w_out):
    nc, S = C.nc, C.S
    from contextlib import ExitStack
    S.fence()
    TB = MTH // 128
    NCH = MTH // 64
    winv = w_in.rearrange("(kc p) f -> p kc f", p=128)
    OQ, OK_, OV, OO, OG = 0, 1024, 2048, 4096, 6144
    with ExitStack() as st:
        def T(name, shape, dt):
            return st.enter_context(nc.sbuf_tensor(C.name(name), shape, dt))
        hT = T("mhT", [128, NDC, MTH], BF16)
        qT = T("mqT", [128, MH, MTH], BF16)
        kT = T("mkT", [128, MH, MTH], BF16)
        Vt = T("mVt", [128, TB, MH, 260], BF16)
        gT = T("mgT", [128, NDC, MTH], BF16)
        Cst = T("mCst", [128, MH, 260], F32)
        Cbf = [T("mCbf", [128, MH, 256], BF16) for _ in range(2)]
        Nb = [T("mNb", [128, MH, 128], BF16) for _ in range(2)]
        mcarry = T("mcarry", [8, 1], F32)
        negbf = T("negbf", [8, 1], F32)
        g_ = {n: T("g_" + n, [128, MTH], F32) for n in ["gi", "ef", "lfp", "nbt", "cf", "c", "cr", "e", "dtok", "earg", "E"]}
        ghi = {n: T("ghi_" + n, [128, MTH], BF16) for n in ["e", "dtok", "E"]}
        glo = {n: T("glo_" + n, [128, MTH], BF16) for n in ["e", "dtok", "E"]}
        for n in ["e", "dtok", "E"]:
            S.I("pool", "memset", ghi[n][:], 0.0, writes=["ghi_" + n])
            S.I("pool", "memset", glo[n][:], 0.0, writes=["glo_" + n])
        s_ = {n: T("s_" + n, [8, NCH], F32) for n in ["off", "cmax", "gneg", "gpos", "mnext", "mprev", "R", "dch", "offR"]}
        S.I("pool", "memset", Cst[:], 0.0, writes=[("Cst", h) for h in range(MH)])
        for i in range(2):
            S.I("pool", "memset", Cbf[i][:], 0.0, writes=[("Cbf", i, h) for h in range(MH)])
            S.I("pool", "memset", Nb[i][:], 0.0, writes=[("Nb", i, h) for h in range(MH)])
        S.I("pool", "memset", mcarry[:], 0.0, writes=["mcarry"])
        S.I("pool", "memset", s_["off"][:], 0.0, writes=["s_off"])
        S.I("pool", "memset", Vt[:, :, :, 256:257], 1.0, writes=[("Vt1",)])
        S.I("dve", "tensor_scalar", out=negbf[:], in0=P["bf"][0:8, :], scalar1=-1.0, scalar2=None, op0=ALU.mult,
            reads=["consts"], writes=["negbf"])
        v3 = lambda t: t[0:8, :].rearrange("p (n l) -> p n l", l=64)
        for ps_i in range(MLSTM_PASSES):
            with ExitStack() as st2:
                big = st2.enter_context(nc.sbuf_tensor(C.name("mbig"), [128, NDC, MTH], F32))
                rstd = st2.enter_context(nc.sbuf_tensor(C.name("mrstd"), [128, MTH], F32))
                tmp = st2.enter_context(nc.sbuf_tensor(C.name("mtmp"), [128, 512], F32))
                sqr = [st2.enter_context(nc.sbuf_tensor(C.name("msq"), [128, 512], BF16)) for _ in range(3)]
                wg = st2.enter_context(nc.sbuf_tensor(C.name("mwg"), [128, NDC, 16], F32))
                S.I("sp", "dma_start", out=wg[:], in_=winv[:, :, OG:OG + 16], writes=["mwg"], dma_key="mwg")
                load_norm_half(C, src, ps_i, big, hT, P["mix_norm"], rstd, tmp, sqr, th=MTH)
                for dc in range(NDC):
                    S.I("dve", "scalar_tensor_tensor", out=big[:, dc, :], in0=big[:, dc, :], scalar=P["mix_norm"][:, dc:dc + 1],
                        in1=rstd[:], op0=ALU.mult, op1=ALU.mult,
                        reads=[("acc", dc), ("rstd", 0), "consts", ("hT", dc, 0)], writes=[("acc", dc)])
                bi_, psi = C.bank()
                bf_, psf = C.bank()
                for (pp, b, c0) in ((psi, bi_, 0), (psf, bf_, 8)):
                    for kc in range(NDC):
                        S.I("pe", "matmul", pp[0:8, :], lhsT=wg[:, kc, c0:c0 + 8], rhs=big[:, kc, :],
                            start=(kc == 0), stop=(kc == NDC - 1), reads=["mwg", ("acc", kc)], writes=[("ps", b)])
                G = lambda n: g_[n][0:8, :]
                S.I("dve", "tensor_scalar", out=G("gi"), in0=psi[0:8, :], scalar1=P["bi"][0:8, :], scalar2=None, op0=ALU.add,
                    reads=[("ps", bi_), "consts"], writes=["g_gi"])
                S.I("act", "activation", out=G("ef"), in_=psf[0:8, :], func=AF.Exp, scale=-1.0, bias=negbf[:],
                    reads=[("ps", bf_), "negbf"], writes=["g_ef"])
                S.I("act", "activation", out=G("lfp"), in_=G("ef"), func=AF.Ln, bias=C.one_col[0:8, :],
                    reads=["g_ef", "consts"], writes=["g_lfp"])
                S.fence()
            S.I("dve", "tensor_tensor_scan", out=G("nbt"), data0=C.ones8[:, 0:MTH], data1=G("lfp"), initial=0.0,
                op0=ALU.mult, op1=ALU.add, reads=["g_lfp", "ones8"], writes=["g_nbt"])
            S.I("dve", "tensor_tensor", out=G("cf"), in0=G("gi"), in1=G("nbt"), op=ALU.add,
                reads=["g_gi", "g_nbt"], writes=["g_cf"])
            S.I("dve", "tensor_copy", out=s_["off"][:, 1:NCH], in_=v3(g_["nbt"])[:, 0:NCH - 1, 63],
                reads=["g_nbt"], writes=["s_off"])
            S.I("dve", "tensor_tensor", out=v3(g_["c"]), in0=v3(g_["cf"]),
                in1=s_["off"][:].unsqueeze(2).to_broadcast([8, NCH, 64]), op=ALU.subtract,
                reads=["g_cf", "s_off"], writes=["g_c"])
            S.I("dve", "tensor_reduce", out=s_["cmax"][:], in_=v3(g_["c"]), axis=mybir.AxisListType.X, op=ALU.max,
                reads=["g_c"], writes=["s_cmax"])
            S.I("dve", "tensor_tensor", out=s_["gneg"][:], in0=v3(g_["nbt"])[:, :, 63], in1=s_["off"][:], op=ALU.subtract,
                reads=["g_nbt", "s_off"], writes=["s_gneg"])
            S.I("dve", "tensor_scalar", out=s_["gpos"][:], in0=s_["gneg"][:], scalar1=-1.0, scalar2=None, op0=ALU.mult,
                reads=["s_gneg"], writes=["s_gpos"])
            S.I("dve", "tensor_tensor_scan", out=s_["mnext"][:], data0=s_["cmax"][:], data1=s_["gpos"][:], initial=mcarry[:],
                op0=ALU.max, op1=ALU.add, reads=["s_cmax", "s_gpos", "mcarry"], writes=["s_mnext"])
            S.I("dve", "tensor_copy", out=s_["mprev"][:, 0:1], in_=mcarry[:], reads=["mcarry"], writes=["s_mprev"])
            S.I("dve", "tensor_copy", out=s_["mprev"][:, 1:NCH], in_=s_["mnext"][:, 0:NCH - 1],
                reads=["s_mnext", "s_mprev"], writes=["s_mprev"])
            S.I("dve", "tensor_copy", out=mcarry[:], in_=s_["mnext"][:, NCH - 1:NCH], reads=["s_mnext", "s_mprev"], writes=["mcarry"])
            S.I("dve", "tensor_tensor", out=s_["R"][:], in0=s_["mprev"][:], in1=s_["cmax"][:], op=ALU.max,
                reads=["s_mprev", "s_cmax"], writes=["s_R"])
            S.I("dve", "tensor_tensor", out=v3(g_["cr"]), in0=v3(g_["c"]),
                in1=s_["R"][:].unsqueeze(2).to_broadcast([8, NCH, 64]), op=ALU.subtract,
                reads=["g_c", "s_R"], writes=["g_cr"])
            S.I("act", "activation", out=G("e"), in_=G("cr"), func=AF.Exp, reads=["g_cr"], writes=["g_e"])
            S.I("dve", "tensor_tensor", out=s_["dch"][:], in0=s_["mprev"][:], in1=s_["R"][:], op=ALU.subtract,
                reads=["s_mprev", "s_R"], writes=["s_dch"])
            S.I("act", "activation", out=s_["dch"][:], in_=s_["dch"][:], func=AF.Exp, reads=["s_dch"], writes=["s_dch"])
            S.I("dve", "tensor_copy", out=v3(g_["dtok"]), in_=s_["dch"][:].unsqueeze(2).to_broadcast([8, NCH, 64]),
                reads=["s_dch"], writes=["g_dtok"])
            S.I("dve", "tensor_tensor", out=s_["offR"][:], in0=s_["off"][:], in1=s_["R"][:], op=ALU.add,
                reads=["s_off", "s_R"], writes=["s_offR"])
            S.I("dve", "tensor_tensor", out=v3(g_["earg"]), in0=v3(g_["nbt"]),
                in1=s_["offR"][:].unsqueeze(2).to_broadcast([8, NCH, 64]), op=ALU.subtract,
                reads=["g_nbt", "s_offR"], writes=["g_earg"])
            S.I("act", "activation", out=G("E"), in_=G("earg"), func=AF.Exp, reads=["g_earg"], writes=["g_E"])
            for n in ["e", "dtok", "E"]:
                S.I("act", "activation", out=ghi[n][0:8, :], in_=G(n), func=AF.Copy, reads=["g_" + n], writes=["ghi_" + n])
                S.I("dve", "tensor_tensor", out=glo[n][0:8, :], in0=G(n), in1=ghi[n][0:8, :], op=ALU.subtract,
                    reads=["g_" + n, "ghi_" + n], writes=["glo_" + n])
            with ExitStack() as st2:
                wq = [st2.enter_context(nc.sbuf_tensor(C.name("mwq"), [128, NDC, 128], BF16)) for _ in range(3)]
                wv = [st2.enter_context(nc.sbuf_tensor(C.name("mwv"), [128, NDC, 512], BF16)) for _ in range(2)]
                cols = [OQ + h * 128 for h in range(MH)] + [OK_ + h * 128 for h in range(MH)]

                def ldq(i):
                    S.I("pool", "dma_start", out=wq[i % 3][:], in_=winv[:, :, cols[i]:cols[i] + 128],
                        writes=[("mwq", i % 3)], dma_key=("mwq", i % 3))

                def ldv(nt):
                    S.I("pool", "dma_start", out=wv[nt % 2][:], in_=winv[:, :, OV + nt * 512:OV + (nt + 1) * 512],
                        writes=[("mwv", nt % 2)], dma_key=("mwv", nt % 2))
                ldq(0)
                ldq(1)
                ldv(0)
                for i in range(2 * MH):
                    if i + 2 < 2 * MH:
                        ldq(i + 2)
                    b, ps = C.bank()
                    for kc in range(NDC):
                        S.I("pe", "matmul", ps[:], lhsT=wq[i % 3][:, kc, :], rhs=hT[:, kc, :], start=(kc == 0), stop=(kc == NDC - 1),
                            reads=[("mwq", i % 3), ("hT", kc, 0)], writes=[("ps", b)])
                    if i < MH:
                        S.I("act", "activation", out=qT[:, i, :], in_=ps[:], func=AF.Copy, scale=float(DK) ** -0.5,
                            reads=[("ps", b)], writes=[("qT", i)])
                    else:
                        S.I("act", "activation", out=kT[:, i - MH, :], in_=ps[:], func=AF.Copy,
                            reads=[("ps", b)], writes=[("kT", i - MH)])
                for nt in range(4):
                    if nt + 1 < 4:
                        ldv(nt + 1)
                    for tb in range(TB):
                        b, ps = C.bank()
                        for kc in range(NDC):
                            S.I("pe", "matmul", ps[:], lhsT=hT[:, kc, tb * 128:(tb + 1) * 128], rhs=wv[nt % 2][:, kc, :],
                                start=(kc == 0), stop=(kc == NDC - 1), reads=[("mwv", nt % 2), ("hT", kc, 0)], writes=[("ps", b)])
                        S.I("act", "activation", out=Vt[:, tb, 2 * nt:2 * nt + 2, 0:256],
                            in_=ps[:].rearrange("p (h d) -> p h d", d=256), func=AF.Copy,
                            reads=[("ps", b)], writes=[("Vt", tb, 2 * nt), ("Vt", tb, 2 * nt + 1)])
                S.fence()
            if 'nocore' in MLSTM_DBG:
                continue
            with ExitStack() as st2:
                def T2(name, shape, dt):
                    return st2.enter_context(nc.sbuf_tensor(C.name(name), shape, dt))
                kw = [T2("kw", [128, MTH], BF16) for _ in range(2)]
                dq = [T2("dq", [128, MTH], BF16) for _ in range(2)]
                dsc = [T2("dsc", [128, NCH], F32) for _ in range(2)]
                Ebc = [T2("Ebc", [128, MTH], F32) for _ in range(2)]
                hbuf = [T2("hbuf", [128, 2, MTH], F32) for _ in range(2)]
                PTm = [T2("PTm", [128, 128], BF16) for _ in range(2)]
                kwt = [T2("kwt", [128, 128], BF16) for _ in range(2)]
                dnm = [T2("dnm", [128, 128], F32) for _ in range(2)]
                rcp = [T2("rcp", [128, 128], F32) for _ in range(2)]
                wog = [T2("wog", [128, NDC, 128], BF16) for _ in range(3)]
                og = [T2("og", [128, MTH], F32) for _ in range(2)]
                hsq = [T2("hsq", [128, MTH], BF16) for _ in range(2)]
                rtm = T2("rtm", [128, MTH], F32)
                rrs = T2("rrs", [128, MTH], F32)
                hn = [T2("hn", [128, MTH], F32) for _ in range(2)]

                def ldo(i):
                    S.I("pool", "dma_start", out=wog[i % 3][:], in_=winv[:, :, OO + i * 128:OO + (i + 1) * 128],
                        writes=[("wog", i % 3)], dma_key=("wog", i % 3))
                ldo(0)
                ldo(1)
                bk = 0
                for h in range(MH):
                    r = h % 2
                    pse, psd, psE = C.banks[0], C.banks[1], C.banks[2]
                    for (pp, b, nm) in ((pse, 0, "e"), (psd, 1, "dtok"), (psE, 2, "E")):
                        S.I("pe", "matmul", pp[:], lhsT=C.sel[:, h, :], rhs=ghi[nm][:], start=True, stop=False,
                            reads=["sel", "ghi_" + nm], writes=[("ps", b)])
                        S.I("pe", "matmul", pp[:], lhsT=C.sel[:, h, :], rhs=glo[nm][:], start=False, stop=True,
                            reads=["sel", "glo_" + nm], writes=[("ps", b)])
                    S.I("dve", "tensor_tensor", out=kw[r][:], in0=kT[:, h, :], in1=pse[:], op=ALU.mult,
                        reads=[("kT", h), ("ps", 0)], writes=[("kw", r)])
                    S.I("dve", "tensor_tensor", out=dq[r][:], in0=qT[:, h, :], in1=psd[:], op=ALU.mult,
                        reads=[("qT", h), ("ps", 1)], writes=[("dq", r)])
                    S.I("act", "activation", out=dsc[r][:], in_=psd[:].rearrange("p (n l) -> p n l", l=64)[:, :, 0],
                        func=AF.Copy, reads=[("ps", 1)], writes=[("dsc", r)])
                    S.I("act", "activation", out=Ebc[r][:], in_=psE[:], func=AF.Copy, reads=[("ps", 2)], writes=[("Ebc", r)])
                    if _chk("c1"):
                        return
                    for tb in range(TB):
                        bs = slice(tb * 128, (tb + 1) * 128)
                        k2 = bk % 2
                        bk += 1
                        pss, pst, psc, psx = C.banks[3], C.banks[4], C.banks[5], C.banks[6]
                        S.I("pe", "matmul", pss[:, 0:128], lhsT=kw[r][:, bs], rhs=qT[:, h, bs], start=True, stop=True,
                            reads=[("kw", r), ("qT", h)], writes=[("ps", 3)])
                        S.I("dve", "tensor_tensor", out=PTm[k2][:], in0=pss[:, 0:128], in1=C.cmask[:], op=ALU.mult,
                            reads=[("ps", 3), "cmask"], writes=[("PTm", k2)])
                        if _chk("c2"):
                            return
                        pstb = pst[:].bitcast(BF16)
                        S.I("pe", "transpose", pstb[:, 0:128], kw[r][:, bs], C.ident[:],
                            reads=[("kw", r), "ident"], writes=[("ps", 4)])
                        S.I("act", "activation", out=kwt[k2][:], in_=pstb[:, 0:128], func=AF.Copy,
                            reads=[("ps", 4)], writes=[("kwt", k2)])
                        if _chk("c3"):
                            return

                        def upd(ci, dst_i):
                            cs = ci * 64
                            n = 2 * tb + ci
                            S.I("pe", "matmul", psc[:, 0:257], lhsT=kwt[k2][cs:cs + 64, :], rhs=Vt[cs:cs + 64, tb, h, 0:257],
                                start=True, stop=True, reads=[("kwt", k2), ("Vt", tb, h), ("Vt1",)], writes=[("ps", 5)])
                            S.I("dve", "scalar_tensor_tensor", out=Cst[:, h, 0:257], in0=Cst[:, h, 0:257],
                                scalar=dsc[r][:, n:n + 1], in1=psc[:, 0:257], op0=ALU.mult, op1=ALU.add,
                                reads=[("Cst", h), ("dsc", r), ("ps", 5)], writes=[("Cst", h)])
                            S.I("act", "activation", out=Cbf[dst_i][:, h, :], in_=Cst[:, h, 0:256], func=AF.Copy,
                                reads=[("Cst", h)], writes=[("Cbf", dst_i, h)])
                            S.I("dve", "tensor_scalar", out=Nb[dst_i][:, h, :], in0=C.ones[:], scalar1=Cst[:, h, 256:257],
                                scalar2=None, op0=ALU.mult, reads=[("Cst", h), "ones"], writes=[("Nb", dst_i, h)])
                        upd(0, 1)
                        if _chk("c4"):
                            return
                        for (c0, lw0, lw1, lwi) in (
                                (0, Cbf[0][:, h, 0:128], Cbf[1][:, h, 0:128], Vt[:, tb, h, 0:128]),
                                (128, Cbf[0][:, h, 128:256], Cbf[1][:, h, 128:256], Vt[:, tb, h, 128:256]),
                                (256, Nb[0][:, h, :], Nb[1][:, h, :], C.ones[:])):
                            rd = [("Vt", tb, h), ("PTm", k2), ("Cbf", 0, h), ("Cbf", 1, h), ("Nb", 0, h), ("Nb", 1, h),
                                  ("dq", r), "ones"]
                            S.I("pe", "matmul", psx[:, c0:c0 + 128], lhsT=lwi, rhs=PTm[k2][:], start=True, stop=False,
                                reads=rd, writes=[("ps", 6)])
                            S.I("pe", "matmul", psx[:, c0:c0 + 64], lhsT=lw0, rhs=dq[r][:, tb * 128:tb * 128 + 64],
                                start=False, stop=False, reads=rd, writes=[("ps", 6)])
                            S.I("pe", "matmul", psx[:, c0 + 64:c0 + 128], lhsT=lw1, rhs=dq[r][:, tb * 128 + 64:tb * 128 + 128],
                                start=False, stop=True, reads=rd, writes=[("ps", 6)])
                        if _chk("c5"):
                            return
                        upd(1, 0)
                        S.I("act", "activation", out=dnm[k2][:], in_=psx[:, 256:384], func=AF.Abs,
                            reads=[("ps", 6)], writes=[("dnm", k2)])
                        S.I("dve", "tensor_tensor", out=dnm[k2][:], in0=dnm[k2][:], in1=Ebc[r][:, bs], op=ALU.max,
                            reads=[("dnm", k2), ("Ebc", r)], writes=[("dnm", k2)])
                        S.I("dve", "reciprocal", out=rcp[k2][:], in_=dnm[k2][:], reads=[("dnm", k2)], writes=[("rcp", k2)])
                        for dvh in range(2):
                            S.I("dve", "tensor_tensor", out=hbuf[r][:, dvh, bs], in0=psx[:, dvh * 128:(dvh + 1) * 128],
                                in1=rcp[k2][:], op=ALU.mult, reads=[("ps", 6), ("rcp", k2)], writes=[("hbuf", r, dvh)])
                    b7 = 7
                    ps7 = C.banks[7]
                    for dvh in range(2):
                        S.I("act", "activation", out=hsq[dvh][:], in_=hbuf[r][:, dvh, :], func=AF.Square,
                            reads=[("hbuf", r, dvh)], writes=[("hsq", dvh)])
                        S.I("pe", "matmul", ps7[:], lhsT=C.ones[:], rhs=hsq[dvh][:], start=(dvh == 0), stop=(dvh == 1),
                            reads=[("hsq", dvh), "ones"], writes=[("ps", 7)])
                    S.I("act", "activation", out=rtm[:], in_=ps7[:], func=AF.Sqrt, scale=1.0 / DV, bias=C.eps_col[:],
                        reads=[("ps", 7), "consts"], writes=["rtm"])
                    S.I("dve", "reciprocal", out=rrs[:], in_=rtm[:], reads=["rtm"], writes=["rrs"])
                    for dvh in range(2):
                        i = 2 * h + dvh
                        if i + 2 < NDC:
                            ldo(i + 2)
                        for kc in range(NDC):
                            S.I("pe", "matmul", ps7[:], lhsT=wog[i % 3][:, kc, :], rhs=hT[:, kc, :],
                                start=(kc == 0), stop=(kc == NDC - 1), reads=[("wog", i % 3), ("hT", kc, 0)], writes=[("ps", 7)])
                        S.I("act", "activation", out=og[dvh][:], in_=ps7[:], func=AF.Sigmoid, bias=P["bo"][:, i:i + 1],
                            reads=[("ps", 7), "consts"], writes=[("og", dvh)])
                        S.I("dve", "scalar_tensor_tensor", out=hn[dvh][:], in0=hbuf[r][:, dvh, :], scalar=P["hnorm"][:, i:i + 1],
                            in1=rrs[:], op0=ALU.mult, op1=ALU.mult, reads=[("hbuf", r, dvh), "rrs", "consts"], writes=[("hn", dvh)])
                        S.I("dve", "tensor_tensor", out=gT[:, i, :], in0=hn[dvh][:], in1=og[dvh][:], op=ALU.mult,
                            reads=[("hn", dvh), ("og", dvh)], writes=[("gT", i)])
                S.fence()
            out_proj_residual(C, src, dst, ps_i, gT, lambda kc, tt: ("gT", kc), w_out, None, th=MTH)
    S.fence()


LAYER_KINDS = ("attn", "conv", "mlstm", "attn")


def _col16(v):
    return np.ascontiguousarray(np.asarray(v, np.float32).reshape(16, 128).T)


def _pad8(v):
    out = np.zeros((128, 1), np.float32)
    out[:8, 0] = np.asarray(v, np.float32)
    return out


def pack_params(inp):
    cols, offs, o = [], {}, 0

    def add(name, a):
        nonlocal o
        a = np.asarray(a, np.float32)
        offs[name] = (o, a.shape[1])
        o += a.shape[1]
        cols.append(a)
    for i, kind in enumerate(LAYER_KINDS):
        p = "l%d_" % i
        add(p + "mix_norm", _col16(inp[p + "mix_norm"]))
        add(p + "mlp_norm", _col16(inp[p + "mlp_norm"]))
        if kind == "attn":
            add(p + "gq", np.tile(np.asarray(inp[p + "attn_q_norm"], np.float32), 2)[:, None])
            add(p + "gk", np.tile(np.asarray(inp[p + "attn_k_norm"], np.float32), 2)[:, None])
            add(p + "sinks", np.ascontiguousarray(np.repeat(np.asarray(inp[p + "attn_sinks"], np.float32).reshape(16, 2), 64, axis=1).T))
        elif kind == "conv":
            b_in = np.asarray(inp[p + "conv_b_in"], np.float32)
            add(p + "b_in_a", _col16(b_in[:D]))
            add(p + "b_in_g", _col16(b_in[D:]))
            dw = np.asarray(inp[p + "conv_dw"], np.float32)
            add(p + "dw", np.ascontiguousarray(dw.T.reshape(16, 128, CONVW).transpose(1, 0, 2).reshape(128, 16 * CONVW)))
            add(p + "dw_b", _col16(inp[p + "conv_dw_b"]))
            add(p + "ln_g", _col16(inp[p + "conv_ln_g"]))
            add(p + "ln_b", _col16(inp[p + "conv_ln_b"]))
            add(p + "b_out", _col16(inp[p + "conv_b_out"]))
        else:
            bg = np.asarray(inp[p + "mlstm_b_gates"], np.float32)
            add(p + "bo", _col16(bg[:D]))
            add(p + "bi", _pad8(bg[D:D + 8]))
            add(p + "bf", _pad8(bg[D + 8:D + 16]))
            add(p + "hnorm", _col16(inp[p + "mlstm_h_norm"]))
    return np.ascontiguousarray(np.concatenate(cols, axis=1)), offs


WSHAPES = {
    "attn_w_qkv": [D, 2560], "attn_w_o": [D, D], "conv_w_in": [D, 2 * D], "conv_w_out": [D, D],
    "mlstm_w_in": [D, 6160], "mlstm_w_out": [D, D], "mlp_w1": [D, DFF], "mlp_w2": [DFF, D],
}
LAYER_W = {"attn": ("attn_w_qkv", "attn_w_o"), "conv": ("conv_w_in", "conv_w_out"), "mlstm": ("mlstm_w_in", "mlstm_w_out")}


def build_program(offs, NP, layers=(0, 1, 2, 3)):
    from contextlib import ExitStack
    nc = bass.Bass("TRN2", target_bir_lowering=False)
    xT = nc.dram_tensor("xT", [D, SEQ], F32, kind="ExternalInput").ap()
    prm = nc.dram_tensor("prm", [128, NP], F32, kind="ExternalInput").ap()
    W = {}
    for i in layers:
        kind = LAYER_KINDS[i]
        for wn in LAYER_W[kind] + ("mlp_w1", "mlp_w2"):
            nm = "l%d_%s" % (i, wn)
            W[nm] = nc.dram_tensor(nm, WSHAPES[wn], F32, kind="ExternalInput").ap()
    yT = nc.dram_tensor("yT", [D, SEQ], F32, kind="ExternalOutput").ap()
    xs = nc.dram_tensor("xs_scratch", [D, SEQ], F32, kind="Internal").ap()
    S = Sched(nc)
    with ExitStack() as st:
        C = Ctx(nc, S, st)
        attn_consts(C)
        mlstm_consts(C)
        pt = st.enter_context(nc.sbuf_tensor("prm_sb", [128, NP], F32))
        S.I("sp", "dma_start", out=pt[:], in_=prm[:, :], writes=["consts"], dma_key="prm")
        cur = xT
        for n, i in enumerate(layers):
            kind = LAYER_KINDS[i]
            p = "l%d_" % i
            P = {k[len(p):]: pt[:, o:o + w] for k, (o, w) in offs.items() if k.startswith(p)}
            wa, wb = LAYER_W[kind]
            if kind == "attn":
                attn_phase(C, cur, xs, P, W[p + wa], W[p + wb])
            elif kind == "conv":
                conv_phase(C, cur, xs, P, W[p + wa], W[p + wb])
            else:
                mlstm_phase(C, cur, xs, P, W[p + wa], W[p + wb])
            last = (n == len(layers) - 1)
            S.fence()
            mlp_phase(C, xs, yT if last else xs, P["mlp_norm"], W[p + "mlp_w1"], W[p + "mlp_w2"])
            S.fence()
            cur = xs
        S.emit()
    return nc, S


def kernel(**inputs):
    x = np.asarray(inputs["x"], np.float32)
    B = x.shape[0]
    prm, offs = pack_params(inputs)
    nc, _ = build_program(offs, prm.shape[1])
    shared = {"prm": prm}
    for i, kind in enumerate(LAYER_KINDS):
        for wn in LAYER_W[kind] + ("mlp_w1", "mlp_w2"):
            nm = "l%d_%s" % (i, wn)
            shared[nm] = np.ascontiguousarray(np.asarray(inputs[nm], np.float32))
    in_maps = []
    for b in range(B):
        m = dict(shared)
        m["xT"] = np.ascontiguousarray(x[b].T)
        in_maps.append(m)
    res = run_bass_kernel_spmd(nc, in_maps, core_ids=list(range(B)))
    out = np.empty_like(x)
    for b in range(B):
        out[b] = np.asarray(res.results[b]["yT"]).T
    return out
```

```python
import numpy as np
import concourse.bass as bass
import concourse.mybir as mybir
from concourse.bass_utils import run_bass_kernel_spmd

F32 = mybir.dt.float32
BF16 = mybir.dt.bfloat16
AF = mybir.ActivationFunctionType
ALU = mybir.AluOpType

SAME_ENG_SYNC = True


class Op:
    __slots__ = ("eng", "fn", "eidx", "signal", "seq", "waits", "dma_key", "dma_cnt", "name")

    def __init__(self, eng, fn, name=""):
        self.eng = eng
        self.fn = fn
        self.eidx = -1
        self.signal = False
        self.seq = -1
        self.waits = []
        self.dma_key = None
        self.dma_cnt = 0
        self.name = name


class Sched:
    ENGS = ("pe", "act", "dve", "pool", "sp")

    def __init__(self, nc):
        self.nc = nc
        self.streams = {e: [] for e in self.ENGS}
        self.last_w = {}
        self.readers = {}
        self.waited = {e: {} for e in self.ENGS}
        self.dma_cnt = {}
        self.dma_last = {}
        self.dma_keys = []
        self.fence_ops = []

    def fence(self):
        f = []
        for e in self.ENGS:
            for o in reversed(self.streams[e]):
                if o.dma_key is None:
                    f.append(o)
                    break
        f.extend(self.dma_last.values())
        self.fence_ops = f

    def _stream_id(self, op):
        return ("dma", op.dma_key) if op.dma_key is not None else op.eng

    def _prog(self, op):
        return op.dma_cnt if op.dma_key is not None else op.eidx

    def op(self, eng, fn, reads=(), writes=(), dma_key=None, name=""):
        o = Op(eng, fn, name)
        o.eidx = len(self.streams[eng])
        deps = {}

        def add(d):
            if d is None:
                return
            sid = self._stream_id(d)
            if sid == eng and d.dma_key is None:
                if eng == "pe" or not SAME_ENG_SYNC:
                    return
            cur = deps.get(sid)
            if cur is None or self._prog(d) > self._prog(cur):
                deps[sid] = d

        if dma_key is not None:
            o.dma_key = dma_key
            if dma_key not in self.dma_cnt:
                self.dma_cnt[dma_key] = 0
                self.dma_keys.append(dma_key)
            add(self.dma_last.get(dma_key))
            self.dma_cnt[dma_key] += 1
            o.dma_cnt = self.dma_cnt[dma_key]
            self.dma_last[dma_key] = o
        for k in reads:
            add(self.last_w.get(k))
        for k in writes:
            add(self.last_w.get(k))
            for r in self.readers.get(k, {}).values():
                add(r)
        for d in self.fence_ops:
            add(d)
        wd = self.waited[eng]
        for sid, d in deps.items():
            p = self._prog(d)
            if wd.get(sid, -1) >= p:
                continue
            wd[sid] = p
            o.waits.append(d)
            if d.dma_key is None:
                d.signal = True
        for k in writes:
            self.last_w[k] = o
            self.readers[k] = {}
        sid = self._stream_id(o)
        for k in reads:
            self.readers.setdefault(k, {})[sid] = o
        self.streams[eng].append(o)
        return o

    def I(self, eng, meth, *args, reads=(), writes=(), dma_key=None, **kw):
        return self.op(eng, lambda e: getattr(e, meth)(*args, **kw), reads, writes, dma_key)

    def emit(self):
        nc = self.nc
        for e in self.ENGS:
            c = 0
            for o in self.streams[e]:
                if o.dma_key is None and o.signal:
                    c += 1
                    o.seq = c
        from contextlib import ExitStack
        with ExitStack() as st:
            esem = {e: st.enter_context(nc.semaphore("s_" + e)) for e in self.ENGS}
            dsem = {k: st.enter_context(nc.semaphore("d_%d" % i)) for i, k in enumerate(self.dma_keys)}
            block = st.enter_context(nc.Block())

            def run(ename, eng):
                for o in self.streams[ename]:
                    for d in o.waits:
                        if d.dma_key is not None:
                            eng.wait_ge(dsem[d.dma_key], 16 * d.dma_cnt)
                        else:
                            eng.wait_ge(esem[d.eng], d.seq)
                    ins = o.fn(eng)
                    if o.dma_key is not None:
                        ins.then_inc(dsem[o.dma_key], 16)
                    elif o.signal:
                        ins.then_inc(esem[ename], 1)
                if ename == "sp":
                    for k in self.dma_keys:
                        eng.wait_ge(dsem[k], 16 * self.dma_cnt[k])

            @block.tensor
            def _(e):
                run("pe", e)

            @block.scalar
            def _(e):
                run("act", e)

            @block.vector
            def _(e):
                run("dve", e)

            @block.gpsimd
            def _(e):
                run("pool", e)

            @block.sync
            def _(e):
                run("sp", e)


D = 2048
SEQ = 2048
DFF = 8192
NDC = D // 128
NFC = DFF // 128
EPS = 1e-6
TH = 1024
NTT = TH // 512


class Ctx:
    def __init__(self, nc, S, st):
        self.nc, self.S, self.st = nc, S, st
        self.psall = st.enter_context(nc.psum_tensor("psall", [128, 8 * 512], F32))
        self.banks = [self.psall[:, i * 512:(i + 1) * 512] for i in range(8)]
        self.bank_i = 0
        self.ones = st.enter_context(nc.sbuf_tensor("ones_bf", [128, 128], BF16))
        S.op("dve", lambda e: e.memset(self.ones[:], 1.0), writes=["ones"])
        self.eps_col = st.enter_context(nc.sbuf_tensor("eps_col", [128, 1], F32))
        S.op("dve", lambda e: e.memset(self.eps_col[:], EPS), writes=["consts"])
        self.uid = 0
        self.dbg = None

    def bank(self):
        b = self.bank_i
        self.bank_i = (b + 1) % 8
        return b, self.banks[b]

    def name(self, s):
        self.uid += 1
        return "%s_%d" % (s, self.uid)


def rms_stats(C, src, src_key, nch, inv_n, sq_ring, rstd_tile, rstd_key, tt, tmp_tile):
    S = C.S
    b, ps = C.bank()
    for dc in range(nch):
        r = dc % len(sq_ring)
        sq = sq_ring[r]
        S.op("act", lambda e, sq=sq, dc=dc: e.activation(out=sq[:], in_=src(dc), func=AF.Square),
             reads=[src_key(dc)], writes=[("sq", r)])
        S.op("pe", lambda e, sq=sq, dc=dc, ps=ps: e.matmul(ps[:], lhsT=C.ones[:], rhs=sq[:],
                                                          start=(dc == 0), stop=(dc == nch - 1)),
             reads=[("sq", r), "ones"], writes=[("ps", b)])
    S.I("act", "activation", out=tmp_tile[:], in_=ps[:], func=AF.Ln, scale=inv_n, bias=C.eps_col[:],
        reads=[("ps", b), "consts"], writes=["rms_tmp"])
    S.I("act", "activation", out=rstd_tile, in_=tmp_tile[:], func=AF.Exp, scale=-0.5, reads=["rms_tmp"], writes=[rstd_key])


def mlp_phase(C, src, dst, gcol, w1, w2):
    nc, S = C.nc, C.S
    from contextlib import ExitStack
    G = 8
    R1, R2 = 3, 10
    with ExitStack() as st:
        acc = st.enter_context(nc.sbuf_tensor(C.name("acc"), [128, NDC, TH], F32))
        hT = st.enter_context(nc.sbuf_tensor(C.name("hT"), [128, NDC, TH], BF16))
        hid = st.enter_context(nc.sbuf_tensor(C.name("hid"), [128, G, TH], BF16))
        w1b = [st.enter_context(nc.sbuf_tensor(C.name("w1b"), [128, NDC, 256], BF16)) for _ in range(R1)]
        w2b = [st.enter_context(nc.sbuf_tensor(C.name("w2b"), [128, D], BF16)) for _ in range(R2)]
        sqr = [st.enter_context(nc.sbuf_tensor(C.name("sq"), [128, 512], BF16)) for _ in range(2)]
        rl = [st.enter_context(nc.sbuf_tensor(C.name("rl"), [128, 512], F32)) for _ in range(3)]
        rstd = st.enter_context(nc.sbuf_tensor(C.name("rstd"), [128, TH], F32))
        tmp = st.enter_context(nc.sbuf_tensor(C.name("rtmp"), [128, 512], F32))
        srcv = src.rearrange("(dc p) t -> p dc t", p=128)
        dstv = dst.rearrange("(dc p) t -> p dc t", p=128)
        w1v = w1.rearrange("(kc p) f -> p kc f", p=128)
        n1 = 0
        n2 = 0
        rli = 0
        for half in range(SEQ // TH):
            t0 = half * TH
            if half == 0:
                for dc in range(NDC):
                    S.I("sp", "dma_start", out=acc[:, dc, :], in_=srcv[:, dc, t0:t0 + TH],
                        writes=[("acc", dc)], dma_key=("accld0", dc % 4))
            def load_w1(j):
                nonlocal n1
                s = n1 % R1
                n1 += 1
                S.op("pool", lambda e, s=s, j=j: e.dma_start(out=w1b[s][:], in_=w1v[:, :, j * 256:(j + 1) * 256]),
                     writes=[("w1b", s)], dma_key=("w1b", s))
                return s

            def load_w2(f):
                nonlocal n2
                s = n2 % R2
                n2 += 1
                S.op("pool", lambda e, s=s, f=f: e.dma_start(out=w2b[s][:], in_=w2[f * 128:(f + 1) * 128, :]),
                     writes=[("w2b", s)], dma_key=("w2b", s))
                return s
            w1slots = {}
            w2slots = {}
            w1slots[0] = load_w1(0)
            w1slots[1] = load_w1(1)
            for tt in range(NTT):
                rms_stats(C, lambda dc, tt=tt: acc[:, dc, tt * 512:(tt + 1) * 512], lambda dc: ("acc", dc), NDC,
                          1.0 / D, sqr, rstd[:, tt * 512:(tt + 1) * 512], ("rstd", tt), tt, tmp)
                for dc in range(NDC):
                    S.op("dve", lambda e, dc=dc, tt=tt: e.scalar_tensor_tensor(
                        out=hT[:, dc, tt * 512:(tt + 1) * 512], in0=acc[:, dc, tt * 512:(tt + 1) * 512],
                        scalar=gcol[:, dc:dc + 1], in1=rstd[:, tt * 512:(tt + 1) * 512],
                        op0=ALU.mult, op1=ALU.mult),
                        reads=[("acc", dc), ("rstd", tt), "consts"], writes=[("hT", dc, tt)])
            for f in range(G):
                w2slots[f] = load_w2(f)
            for g in range(NFC // G):
                for fl in range(G):
                    f = g * G + fl
                    j = f // 2
                    if f % 2 == 0 and j + 2 < NFC // 2:
                        w1slots[j + 2] = load_w1(j + 2)
                    s1 = w1slots[j]
                    c0 = (f % 2) * 128
                    for tt in range(NTT):
                        b, ps = C.bank()
                        for kc in range(NDC):
                            S.op("pe", lambda e, ps=ps, s1=s1, c0=c0, kc=kc, tt=tt: e.matmul(
                                ps[:], lhsT=w1b[s1][:, kc, c0:c0 + 128], rhs=hT[:, kc, tt * 512:(tt + 1) * 512],
                                start=(kc == 0), stop=(kc == NDC - 1)),
                                reads=[("w1b", s1), ("hT", kc, tt)], writes=[("ps", b)])
                        r = rli % len(rl)
                        rli += 1
                        S.op("act", lambda e, ps=ps, r=r: e.activation(out=rl[r][:], in_=ps[:], func=AF.Relu),
                             reads=[("ps", b)], writes=[("rl", r)])
                        S.op("dve", lambda e, r=r, fl=fl, tt=tt: e.tensor_tensor(
                            out=hid[:, fl, tt * 512:(tt + 1) * 512], in0=rl[r][:], in1=rl[r][:], op=ALU.mult),
                            reads=[("rl", r)], writes=[("hid", fl, tt)])
                nxt = [(g + 1) * G + fl for fl in range(G)] if g + 1 < NFC // G else []
                pre = nxt[:R2 - G]
                for f in pre:
                    w2slots[f] = load_w2(f)
                rest = nxt[R2 - G:]
                for dc in range(NDC):
                    for tt in range(NTT):
                        b, ps = C.bank()
                        for fl in range(G):
                            s2 = w2slots[g * G + fl]
                            S.op("pe", lambda e, ps=ps, s2=s2, dc=dc, fl=fl, tt=tt: e.matmul(
                                ps[:], lhsT=w2b[s2][:, dc * 128:(dc + 1) * 128], rhs=hid[:, fl, tt * 512:(tt + 1) * 512],
                                start=(fl == 0), stop=(fl == G - 1)),
                                reads=[("w2b", s2), ("hid", fl, tt)], writes=[("ps", b)])
                        S.op("dve", lambda e, ps=ps, dc=dc, tt=tt: e.tensor_tensor(
                            out=acc[:, dc, tt * 512:(tt + 1) * 512], in0=acc[:, dc, tt * 512:(tt + 1) * 512],
                            in1=ps[:], op=ALU.add),
                            reads=[("ps", b), ("acc", dc)], writes=[("acc", dc)])
                    if g == NFC // G - 1:
                        S.I("sp", "dma_start", out=dstv[:, dc, t0:t0 + TH], in_=acc[:, dc, :],
                            reads=[("acc", dc)], dma_key=("accst", dc % 4))
                        if half + 1 < SEQ // TH:
                            S.I("pool", "dma_start", out=acc[:, dc, :], in_=srcv[:, dc, t0 + TH:t0 + 2 * TH],
                                writes=[("acc", dc)], dma_key=("accld", dc % 4))
                for f in rest:
                    w2slots[f] = load_w2(f)


def load_norm_half(C, src, half, big, hT, gcol, rstd, tmp, sqr, th=TH):
    nc, S = C.nc, C.S
    srcv = src.rearrange("(dc p) t -> p dc t", p=128)
    t0 = half * th
    for q in range(4):
        S.op("sp", lambda e, q=q: e.dma_start(out=big[:, 4 * q:4 * q + 4, :], in_=srcv[:, 4 * q:4 * q + 4, t0:t0 + th]),
             writes=[("acc", dc) for dc in range(4 * q, 4 * q + 4)], dma_key=("acc", q))
    for tt in range(th // 512):
        sl = slice(tt * 512, (tt + 1) * 512)
        rms_stats(C, lambda dc, sl=sl: big[:, dc, sl], lambda dc: ("acc", dc), NDC, 1.0 / D, sqr,
                  rstd[:, sl], ("rstd", tt), tt, tmp)
        for dc in range(NDC):
            S.op("dve", lambda e, dc=dc, sl=sl: e.scalar_tensor_tensor(
                out=hT[:, dc, sl], in0=big[:, dc, sl], scalar=gcol[:, dc:dc + 1], in1=rstd[:, sl],
                op0=ALU.mult, op1=ALU.mult),
                reads=[("acc", dc), ("rstd", tt), "consts"], writes=[("hT", dc, tt)])


def out_proj_residual(C, src, dst, half, actT, act_key, w_out, bias_col, th=TH):
    nc, S = C.nc, C.S
    from contextlib import ExitStack
    srcv = src.rearrange("(dc p) t -> p dc t", p=128)
    dstv = dst.rearrange("(dc p) t -> p dc t", p=128)
    wv = w_out.rearrange("(kc p) f -> p kc f", p=128)
    t0 = half * th
    with ExitStack() as st:
        wob = [st.enter_context(nc.sbuf_tensor(C.name("wob"), [128, NDC, 128], BF16)) for _ in range(3)]
        xres = [st.enter_context(nc.sbuf_tensor(C.name("xres"), [128, th], F32)) for _ in range(3)]
        ob = [st.enter_context(nc.sbuf_tensor(C.name("ob"), [128, th], F32)) for _ in range(2)]

        def ld(dc):
            s = dc % 3
            S.op("pool", lambda e: e.dma_start(out=wob[s][:], in_=wv[:, :, dc * 128:(dc + 1) * 128]),
                 writes=[("wob", s)], dma_key=("wob", s))
            S.op("sp", lambda e: e.dma_start(out=xres[dc % 3][:], in_=srcv[:, dc, t0:t0 + th]),
                 writes=[("xres", dc % 3)], dma_key=("xres", dc % 3))
        ld(0)
        ld(1)
        for dc in range(NDC):
            if dc + 2 < NDC:
                ld(dc + 2)
            s = dc % 3
            for tt in range(th // 512):
                sl = slice(tt * 512, (tt + 1) * 512)
                b, ps = C.bank()
                for kc in range(NDC):
                    S.op("pe", lambda e, ps=ps, kc=kc, sl=sl, s=s: e.matmul(ps[:], lhsT=wob[s][:, kc, :], rhs=actT[:, kc, sl],
                                                                      start=(kc == 0), stop=(kc == NDC - 1)),
                         reads=[("wob", s), act_key(kc, tt)], writes=[("ps", b)])
                if bias_col is not None:
                    S.op("dve", lambda e, ps=ps, sl=sl, dc=dc: e.scalar_tensor_tensor(
                        out=ob[dc % 2][:, sl], in0=ps[:], scalar=bias_col[:, dc:dc + 1], in1=xres[dc % 3][:, sl],
                        op0=ALU.add, op1=ALU.add),
                        reads=[("ps", b), ("xres", dc % 3), "consts"], writes=[("ob", dc % 2, tt)])
                else:
                    S.op("dve", lambda e, ps=ps, sl=sl, dc=dc: e.tensor_tensor(
                        out=ob[dc % 2][:, sl], in0=ps[:], in1=xres[dc % 3][:, sl], op=ALU.add),
                        reads=[("ps", b), ("xres", dc % 3)], writes=[("ob", dc % 2, tt)])
            S.op("sp", lambda e, dc=dc: e.dma_start(out=dstv[:, dc, t0:t0 + th], in_=ob[dc % 2][:]),
                 reads=[("ob", dc % 2, tt) for tt in range(th // 512)], dma_key=("obst", dc % 2))
        S.fence()


CONVW = 31


def conv_phase(C, src, dst, P, w_in, w_out):
    nc, S = C.nc, C.S
    from contextlib import ExitStack
    S.fence()
    with ExitStack() as st:
        big = st.enter_context(nc.sbuf_tensor(C.name("cbig"), [128, NDC, TH], F32))
        hT = st.enter_context(nc.sbuf_tensor(C.name("chT"), [128, NDC, TH], BF16))
        halo = st.enter_context(nc.sbuf_tensor(C.name("halo"), [128, NDC, CONVW - 1], BF16))
        rstd = st.enter_context(nc.sbuf_tensor(C.name("crstd"), [128, TH], F32))
        tmp = st.enter_context(nc.sbuf_tensor(C.name("ctmp"), [128, 512], F32))
        sqr = [st.enter_context(nc.sbuf_tensor(C.name("csq"), [128, 512], BF16)) for _ in range(3)]
        win_v = w_in.rearrange("(kc p) f -> p kc f", p=128)
        for half in range(SEQ // TH):
            load_norm_half(C, src, half, big, hT, P["mix_norm"], rstd, tmp, sqr)
            with ExitStack() as st2:
                wib = [st2.enter_context(nc.sbuf_tensor(C.name("wib"), [128, NDC, 256], BF16)) for _ in range(3)]
                ub = [st2.enter_context(nc.sbuf_tensor(C.name("ub"), [128, CONVW - 1 + TH], BF16)) for _ in range(3)]
                sg = [st2.enter_context(nc.sbuf_tensor(C.name("sg"), [128, 512], F32)) for _ in range(2)]
                dg = [st2.enter_context(nc.sbuf_tensor(C.name("dg"), [128, CONVW, 128], BF16)) for _ in range(2)]

                def ldw(c):
                    s = c % 3
                    S.I("pool", "dma_start", out=wib[s][:, :, 0:128], in_=win_v[:, :, c * 128:(c + 1) * 128],
                        writes=[("wib", s, 0)], dma_key=("wib", s, 0))
                    S.I("pool", "dma_start", out=wib[s][:, :, 128:256], in_=win_v[:, :, D + c * 128:D + (c + 1) * 128],
                        writes=[("wib", s, 1)], dma_key=("wib", s, 1))

                def proj(c):
                    s = c % 3
                    u = ub[c % 3]
                    uk = ("ub", c % 3)
                    if half == 0:
                        S.I("pool", "memset", u[:, 0:CONVW - 1], 0.0, writes=[uk])
                    else:
                        S.I("pool", "tensor_copy", out=u[:, 0:CONVW - 1], in_=halo[:, c, :], reads=[("halo", c)], writes=[uk])
                    S.I("dve", "tensor_tensor", out=dg[c % 2][:],
                        in0=C.ident[:].unsqueeze(1).to_broadcast([128, CONVW, 128]),
                        in1=P["dw"][:, c * CONVW:(c + 1) * CONVW].unsqueeze(2).to_broadcast([128, CONVW, 128]),
                        op=ALU.mult, reads=["ident", "consts"], writes=[("dg", c % 2)])
                    for tt in range(NTT):
                        sl = slice(tt * 512, (tt + 1) * 512)
                        ba, psa = C.bank()
                        for kc in range(NDC):
                            S.I("pe", "matmul", psa[:], lhsT=wib[s][:, kc, 0:128], rhs=hT[:, kc, sl], start=(kc == 0),
                                stop=(kc == NDC - 1), reads=[("wib", s, 0), ("hT", kc, tt)], writes=[("ps", ba)])
                        bg, psg = C.bank()
                        for kc in range(NDC):
                            S.I("pe", "matmul", psg[:], lhsT=wib[s][:, kc, 128:256], rhs=hT[:, kc, sl], start=(kc == 0),
                                stop=(kc == NDC - 1), reads=[("wib", s, 1), ("hT", kc, tt)], writes=[("ps", bg)])
                        sgt = sg[tt % 2]
                        sgk = ("sg", tt % 2)
                        S.I("act", "activation", out=sgt[:], in_=psg[:], func=AF.Sigmoid, bias=P["b_in_g"][:, c:c + 1],
                            reads=[("ps", bg), "consts"], writes=[sgk])
                        S.I("dve", "scalar_tensor_tensor", out=u[:, CONVW - 1 + tt * 512:CONVW - 1 + (tt + 1) * 512], in0=psa[:],
                            scalar=P["b_in_a"][:, c:c + 1], in1=sgt[:], op0=ALU.add, op1=ALU.mult,
                            reads=[("ps", ba), sgk, "consts", uk], writes=[uk])
                    S.I("pool", "tensor_copy", out=halo[:, c, :], in_=u[:, TH:TH + CONVW - 1], reads=[uk], writes=[("halo", c)])

                def conv(c):
                    u = ub[c % 3]
                    uk = ("ub", c % 3)
                    for tt in range(NTT):
                        b, ps = C.bank()
                        for j in range(CONVW):
                            S.I("pe", "matmul", ps[:], lhsT=dg[c % 2][:, j, :], rhs=u[:, tt * 512 + j:tt * 512 + j + 512],
                                start=(j == 0), stop=(j == CONVW - 1), reads=[("dg", c % 2), uk], writes=[("ps", b)])
                        S.I("act", "activation", out=big[:, c, tt * 512:(tt + 1) * 512], in_=ps[:], func=AF.Identity,
                            bias=P["dw_b"][:, c:c + 1], reads=[("ps", b), "consts"], writes=[("acc", c)])
                ldw(0)
                ldw(1)
                for c in range(NDC + 1):
                    if c < NDC:
                        if c + 2 < NDC:
                            ldw(c + 2)
                        proj(c)
                    if c >= 1:
                        conv(c - 1)
                S.fence()
            with ExitStack() as st2:
                vb = [st2.enter_context(nc.sbuf_tensor(C.name("vb"), [128, 512], BF16)) for _ in range(3)]
                vq = [st2.enter_context(nc.sbuf_tensor(C.name("vq"), [128, 512], BF16)) for _ in range(3)]
                mu = st2.enter_context(nc.sbuf_tensor(C.name("mu"), [128, 512], F32))
                musq = st2.enter_context(nc.sbuf_tensor(C.name("musq"), [128, 512], F32))
                var = st2.enter_context(nc.sbuf_tensor(C.name("var"), [128, 512], F32))
                rs = st2.enter_context(nc.sbuf_tensor(C.name("lrs"), [128, 512], F32))
                t1 = [st2.enter_context(nc.sbuf_tensor(C.name("t1"), [128, 512], F32)) for _ in range(2)]
                t2 = [st2.enter_context(nc.sbuf_tensor(C.name("t2"), [128, 512], F32)) for _ in range(2)]
                for tt in range(NTT):
                    sl = slice(tt * 512, (tt + 1) * 512)
                    b1, ps1 = C.bank()
                    b2, ps2 = C.bank()
                    for c in range(NDC):
                        r = c % 3
                        S.I("act", "activation", out=vb[r][:], in_=big[:, c, sl], func=AF.Copy, reads=[("acc", c)], writes=[("vb", r)])
                        S.I("pe", "matmul", ps1[:], lhsT=C.ones[:], rhs=vb[r][:], start=(c == 0), stop=(c == NDC - 1),
                            reads=[("vb", r), "ones"], writes=[("ps", b1)])
                        S.I("act", "activation", out=vq[r][:], in_=big[:, c, sl], func=AF.Square, reads=[("acc", c)], writes=[("vq", r)])
                        S.I("pe", "matmul", ps2[:], lhsT=C.ones[:], rhs=vq[r][:], start=(c == 0), stop=(c == NDC - 1),
                            reads=[("vq", r), "ones"], writes=[("ps", b2)])
                    S.I("act", "activation", out=mu[:], in_=ps1[:], func=AF.Copy, scale=1.0 / D, reads=[("ps", b1)], writes=["mu"])
                    S.I("act", "activation", out=musq[:], in_=ps1[:], func=AF.Square, scale=1.0 / D, reads=[("ps", b1)], writes=["musq"])
                    S.I("dve", "scalar_tensor_tensor", out=var[:], in0=ps2[:], scalar=1.0 / D, in1=musq[:], op0=ALU.mult,
                        op1=ALU.subtract, reads=[("ps", b2), "musq"], writes=["var"])
                    S.I("act", "activation", out=musq[:], in_=var[:], func=AF.Ln, bias=C.eps_col[:],
                        reads=["var", "consts"], writes=["musq"])
                    S.I("act", "activation", out=rs[:], in_=musq[:], func=AF.Exp, scale=-0.5, reads=["musq"], writes=["lrs"])
                    for c in range(NDC):
                        r = c % 2
                        S.I("pool", "tensor_tensor", out=t1[r][:], in0=big[:, c, sl], in1=mu[:], op=ALU.subtract,
                            reads=[("acc", c), "mu"], writes=[("t1", r)])
                        S.I("dve", "tensor_tensor", out=t2[r][:], in0=t1[r][:], in1=rs[:], op=ALU.mult,
                            reads=[("t1", r), "lrs"], writes=[("t2", r)])
                        S.I("act", "activation", out=hT[:, c, sl], in_=t2[r][:], func=AF.Silu, scale=P["ln_g"][:, c:c + 1],
                            bias=P["ln_b"][:, c:c + 1], reads=[("t2", r), "consts"], writes=[("hT", c, tt)])
                S.fence()
            if C.dbg is not None and half == 0:
                S.op("sp", lambda e: e.dma_start(out=C.dbg["v"], in_=big[:]), reads=[("acc", c) for c in range(NDC)], dma_key="dbgv")
                S.op("sp", lambda e: e.dma_start(out=C.dbg["s"], in_=hT[:]), reads=[("hT", c, tt) for c in range(NDC) for tt in range(NTT)], dma_key="dbgs")
            out_proj_residual(C, src, dst, half, hT, lambda kc, tt: ("hT", kc, tt), w_out, P["b_out"])
    S.fence()


NEGBIG = -30000.0


def attn_consts(C):
    nc, S, st = C.nc, C.S, C.st
    C.bdones = st.enter_context(nc.sbuf_tensor("bdones", [128, 128], BF16))
    C.amask = st.enter_context(nc.sbuf_tensor("amask", [128, 128], BF16))
    C.bmask = st.enter_context(nc.sbuf_tensor("bmask", [128, 256], BF16))
    S.I("pool", "memset", C.bdones[:], 0.0, writes=["bdones"])
    S.I("pool", "memset", C.bdones[0:64, 0:64], 1.0, reads=["bdones"], writes=["bdones"])
    S.I("pool", "memset", C.bdones[64:128, 64:128], 1.0, reads=["bdones"], writes=["bdones"])
    S.I("pool", "memset", C.amask[:], 0.0, writes=["amask"])
    S.I("pool", "memset", C.bmask[:], 0.0, writes=["bmask"])
    for base in (0, 64):
        S.I("pool", "memset", C.amask[base:base + 1, 64:128], 1.0, reads=["amask"], writes=["amask"])
        S.I("pool", "memset", C.amask[base + 32:base + 33, 0:64], 1.0, reads=["amask"], writes=["amask"])
        S.I("pool", "memset", C.bmask[base:base + 1, 0:64], NEGBIG, reads=["bmask"], writes=["bmask"])
        S.I("pool", "memset", C.bmask[base + 32:base + 33, 192:256], NEGBIG, reads=["bmask"], writes=["bmask"])


def head_rms_evac(C, ps, b, gcol, out_ap, out_key, sq, sqk, tmp, tmpk, rs, rsk, inv_n, ones_lhsT):
    S = C.S
    S.I("act", "activation", out=sq[:], in_=ps[:], func=AF.Square, reads=[("ps", b)], writes=[sqk])

    def finish():
        b2, ps2 = C.bank()
        S.I("pe", "matmul", ps2[:], lhsT=ones_lhsT, rhs=sq[:], start=True, stop=True,
            reads=[sqk, "bdones", "ones"], writes=[("ps", b2)])
        S.I("act", "activation", out=tmp[:], in_=ps2[:], func=AF.Ln, scale=inv_n, bias=C.eps_col[:],
            reads=[("ps", b2), "consts"], writes=[tmpk])
        S.I("act", "activation", out=rs[:], in_=tmp[:], func=AF.Exp, scale=-0.5, reads=[tmpk], writes=[rsk])
        S.I("dve", "scalar_tensor_tensor", out=out_ap, in0=ps[:], scalar=gcol, in1=rs[:], op0=ALU.mult, op1=ALU.mult,
            reads=[("ps", b), rsk, "consts"], writes=[out_key])
    return finish


def attn_phase(C, src, dst, P, w_qkv, w_o):
    nc, S = C.nc, C.S
    from contextlib import ExitStack
    S.fence()
    NB = TH // 128
    with ExitStack() as st:
        hT = st.enter_context(nc.sbuf_tensor(C.name("ahT"), [128, NDC, TH], BF16))
        qT = st.enter_context(nc.sbuf_tensor(C.name("aqT"), [128, NDC, TH], BF16))
        kT = [st.enter_context(nc.sbuf_tensor(C.name("akT"), [128, 128 + TH], BF16)) for _ in range(4)]
        V = st.enter_context(nc.sbuf_tensor(C.name("aV"), [128, NB + 1, 256], BF16))
        esink = st.enter_context(nc.sbuf_tensor(C.name("esink"), [128, NDC], F32))
        S.I("act", "activation", out=esink[:], in_=P["sinks"], func=AF.Exp, reads=["consts"], writes=["esink"])
        wv_ = w_qkv.rearrange("(kc p) f -> p kc f", p=128)
        for half in range(SEQ // TH):
            with ExitStack() as st2:
                big = st2.enter_context(nc.sbuf_tensor(C.name("abig"), [128, NDC, TH], F32))
                rstd = st2.enter_context(nc.sbuf_tensor(C.name("arstd"), [128, TH], F32))
                tmp = st2.enter_context(nc.sbuf_tensor(C.name("atmp"), [128, 512], F32))
                sqr = [st2.enter_context(nc.sbuf_tensor(C.name("asq"), [128, 512], BF16)) for _ in range(3)]
                load_norm_half(C, src, half, big, hT, P["mix_norm"], rstd, tmp, sqr)
                S.fence()
            with ExitStack() as st2:
                wq = [st2.enter_context(nc.sbuf_tensor(C.name("wq"), [128, NDC, 128], BF16)) for _ in range(3)]
                wk = [st2.enter_context(nc.sbuf_tensor(C.name("wk"), [128, NDC, 128], BF16)) for _ in range(2)]
                wvt = st2.enter_context(nc.sbuf_tensor(C.name("wvt"), [128, NDC, 256], BF16))
                sq = [st2.enter_context(nc.sbuf_tensor(C.name("bsq"), [128, 512], BF16)) for _ in range(3)]
                tm = [st2.enter_context(nc.sbuf_tensor(C.name("btm"), [128, 512], F32)) for _ in range(3)]
                rs = [st2.enter_context(nc.sbuf_tensor(C.name("brs"), [128, 512], F32)) for _ in range(3)]
                ei = 0
                pend = [None]

                def ldq(c):
                    s = c % 3
                    S.I("pool", "dma_start", out=wq[s][:], in_=wv_[:, :, c * 128:(c + 1) * 128],
                        writes=[("wq", s)], dma_key=("wq", s))

                def ldk(g):
                    s = g % 2
                    for d2 in range(2):
                        S.I("pool", "dma_start", out=wk[s][:, :, d2 * 64:(d2 + 1) * 64],
                            in_=wv_[:, :, D + g * 64:D + (g + 1) * 64], writes=[("wk", s, d2)], dma_key=("wk", s, d2))
                S.I("pool", "dma_start", out=wvt[:], in_=wv_[:, :, D + 256:D + 512], writes=["wvt"], dma_key="wvt")
                ldk(0)
                ldk(1)
                ldq(0)
                ldq(1)
                if half > 0:
                    for g in range(4):
                        S.I("pool", "tensor_copy", out=kT[g][:, 0:128], in_=kT[g][:, TH:TH + 128],
                            reads=[("kT", g, "w1")], writes=[("kT", g, 0)])
                    S.I("pool", "tensor_copy", out=V[:, 0, :], in_=V[:, NB, :], reads=[("V", NB)], writes=[("V", 0)])
                for tb in range(NB):
                    b, ps = C.bank()
                    for kc in range(NDC):
                        S.I("pe", "matmul", ps[:, 0:256], lhsT=hT[:, kc, tb * 128:(tb + 1) * 128], rhs=wvt[:, kc, :],
                            start=(kc == 0), stop=(kc == NDC - 1), reads=[("hT", kc, tb // 4), "wvt"], writes=[("ps", b)])
                    S.I("act", "activation", out=V[:, tb + 1, :], in_=ps[:, 0:256], func=AF.Copy,
                        reads=[("ps", b)], writes=[("V", tb + 1)])
                for g in range(4):
                    if g + 2 < 4:
                        pass
                    s = g % 2
                    for tt in range(NTT):
                        b, ps = C.bank()
                        for kc in range(NDC):
                            S.I("pe", "matmul", ps[:], lhsT=wk[s][:, kc, :], rhs=hT[:, kc, tt * 512:(tt + 1) * 512],
                                start=(kc == 0), stop=(kc == NDC - 1),
                                reads=[("wk", s, 0), ("wk", s, 1), ("hT", kc, tt)], writes=[("ps", b)])
                        e2 = ei % 3
                        ei += 1
                        fin = head_rms_evac(C, ps, b, P["gk"], kT[g][:, 128 + tt * 512:128 + (tt + 1) * 512],
                                            ("kT", g, "w%d" % tt), sq[e2], ("bsq", e2), tm[e2], ("btm", e2), rs[e2], ("brs", e2),
                                            1.0 / 64, C.bdones[:])
                        if pend[0] is not None:
                            pend[0]()
                        pend[0] = fin
                    if g + 2 < 4:
                        ldk(g + 2)
                for c in range(NDC):
                    if c + 2 < NDC:
                        ldq(c + 2)
                    s = c % 3
                    for tt in range(NTT):
                        b, ps = C.bank()
                        for kc in range(NDC):
                            S.I("pe", "matmul", ps[:], lhsT=wq[s][:, kc, :], rhs=hT[:, kc, tt * 512:(tt + 1) * 512],
                                start=(kc == 0), stop=(kc == NDC - 1), reads=[("wq", s), ("hT", kc, tt)], writes=[("ps", b)])
                        e2 = ei % 3
                        ei += 1
                        fin = head_rms_evac(C, ps, b, P["gq"], qT[:, c, tt * 512:(tt + 1) * 512], ("qT", c, tt),
                                            sq[e2], ("bsq", e2), tm[e2], ("btm", e2), rs[e2], ("brs", e2), 1.0 / 64, C.bdones[:])
                        if pend[0] is not None:
                            pend[0]()
                        pend[0] = fin
                pend[0]()
                S.fence()
            with ExitStack() as st2:
                NPT = 4
                PT = [st2.enter_context(nc.sbuf_tensor(C.name("PT"), [128, 2, 256], BF16)) for _ in range(NPT)]
                PTp = [st2.enter_context(nc.sbuf_tensor(C.name("PTp"), [128, 2, 128], BF16)) for _ in range(2)]
                PTo = [st2.enter_context(nc.sbuf_tensor(C.name("PTo"), [128, 2, 128], BF16)) for _ in range(2)]
                for i in range(NPT):
                    S.I("pool", "memset", PT[i][:], 0.0, writes=[("PT", i, 0), ("PT", i, 1)])
                for i in range(2):
                    S.I("pool", "memset", PTp[i][:], 0.0, writes=[("PTp", i, 0), ("PTp", i, 1)])
                    S.I("pool", "memset", PTo[i][:], 0.0, writes=[("PTo", i, 0), ("PTo", i, 1)])
                dn = [st2.enter_context(nc.sbuf_tensor(C.name("dn"), [128, 512], F32)) for _ in range(2)]
                rc = [st2.enter_context(nc.sbuf_tensor(C.name("rc"), [128, 512], F32)) for _ in range(2)]
                st_ = {"pti": 0, "nrm": 0, "sbi": 0, "nbi": 0, "bn": None, "bd": None}
                sbanks = [0, 1, 2, 3]
                nbanks = [(4, 5), (6, 7)]

                def emit_scores(c, j):
                    g = c // 4
                    kb = j + 1
                    if j == -1:
                        q0, n = 0, 128
                    elif j == NB - 1:
                        q0, n = j * 128, 128
                    else:
                        q0, n = j * 128, 256
                    cur = {}
                    b = sbanks[2 * (st_["sbi"] % 2)]
                    st_["sbi"] += 1
                    tts = sorted({q0 // 512, (q0 + n - 1) // 512})
                    for hi, base in enumerate((0, 64)):
                        S.I("pe", "matmul", C.banks[b + hi][:, 0:n], lhsT=kT[g][base:base + 64, kb * 128:(kb + 1) * 128],
                            rhs=qT[base:base + 64, c, q0:q0 + n], start=True, stop=True,
                            reads=[("kT", g, 0), ("kT", g, "w0"), ("kT", g, "w1")] + [("qT", c, t) for t in tts],
                            writes=[("ps", b + hi)])
                    pti = st_["pti"]
                    st_["pti"] += 1
                    if j == -1:
                        pt = pti % 2
                        ptile, pkey = PTp[pt], ("PTp", pt)
                        regs = ((0, 64, 0, 64), (64, 128, 0, 128))
                    elif j == NB - 1:
                        pt = pti % 2
                        ptile, pkey = PTo[pt], ("PTo", pt)
                        regs = ((0, 64, 0, 128), (64, 128, 64, 128))
                    else:
                        pt = pti % NPT
                        ptile, pkey = PT[pt], ("PT", pt)
                        regs = ((0, 64, 0, 192), (64, 128, 64, 256))
                    psv = C.psall[:, b * 512:(b + 2) * 512].rearrange("p (h q) -> p h q", h=2)
                    for ri, (p0, p1, c0_, c1_) in enumerate(regs):
                        S.I("act", "activation", out=ptile[p0:p1, :, c0_:c1_], in_=psv[p0:p1, :, c0_:c1_], func=AF.Exp, scale=0.125,
                            reads=[("ps", b), ("ps", b + 1)], writes=[pkey + (ri,)])
                    for hi, base in enumerate((0, 64)):
                        cur[base] = (ptile[:, hi, :], pkey, n)
                    return cur

                def emit_pv(c, j, prevPT, curPT):
                    g = c // 4
                    kb = j + 1
                    if j % 4 == 0:
                        st_["bn"], st_["bd"] = nbanks[st_["nbi"] % 2]
                        st_["nbi"] += 1
                    bn, bd = st_["bn"], st_["bd"]
                    cols = slice((j % 4) * 128, (j % 4 + 1) * 128)
                    for (bk, lw) in ((bn, None), (bd, C.ones[:, 0:64])):
                        for base in (0, 64):
                            have_prev = prevPT is not None
                            if have_prev:
                                ptl, pky, pn = prevPT[base]
                                pc = slice(128, 256) if pn == 256 else slice(0, 128)
                                S.I("pe", "matmul", C.banks[bk][base:base + 64, cols],
                                    lhsT=(V[:, kb - 1, g * 64:(g + 1) * 64] if lw is None else lw),
                                    rhs=ptl[:, pc], start=True, stop=False,
                                    reads=[("V", kb - 1), pky + (0,), pky + (1,), "ones"], writes=[("ps", bk)])
                            ctl, cky, _n = curPT[base]
                            S.I("pe", "matmul", C.banks[bk][base:base + 64, cols],
                                lhsT=(V[:, kb, g * 64:(g + 1) * 64] if lw is None else lw),
                                rhs=ctl[:, 0:128], start=(not have_prev), stop=True,
                                reads=[("V", kb), cky + (0,), cky + (1,), "ones"], writes=[("ps", bk)])
                    if j % 4 == 3:
                        r = st_["nrm"] % 2
                        st_["nrm"] += 1
                        ocols = slice((j - 3) * 128, (j + 1) * 128)
                        S.I("dve", "tensor_scalar", out=dn[r][:], in0=C.banks[bd][:], scalar1=esink[:, c:c + 1],
                            scalar2=None, op0=ALU.add, reads=[("ps", bd), "esink"], writes=[("dn", r)])
                        S.I("dve", "reciprocal", out=rc[r][:], in_=dn[r][:], reads=[("dn", r)], writes=[("rc", r)])
                        S.I("dve", "tensor_tensor", out=hT[:, c, ocols], in0=C.banks[bn][:], in1=rc[r][:], op=ALU.mult,
                            reads=[("ps", bn), ("rc", r)], writes=[("hT", c, j // 4)])

                steps = [(c, j) for c in range(NDC) for j in range(-1, NB) if not (j == -1 and half == 0)]
                pts = {}
                pts[steps[0]] = emit_scores(*steps[0])
                for si, (c, j) in enumerate(steps):
                    if si + 1 < len(steps):
                        pts[steps[si + 1]] = emit_scores(*steps[si + 1])
                    if j >= 0:
                        emit_pv(c, j, pts.get((c, j - 1)), pts[(c, j)])
                    pts.pop((c, j - 1), None)
                S.fence()
            out_proj_residual(C, src, dst, half, hT, lambda kc, tt: ("hT", kc, tt), w_o, None)
    S.fence()


MTH = 512
MLSTM_PASSES = 2048 // 512
MLSTM_DBG = ''


class _Stop(Exception):
    pass


def _chk(tag):
    return MLSTM_DBG == tag
MH = 8
DK = 128
DV = 256


def mlstm_consts(C):
    nc, S, st = C.nc, C.S, C.st
    C.one_col = st.enter_context(nc.sbuf_tensor("one_col", [128, 1], F32))
    S.I("pool", "memset", C.one_col[:], 1.0, writes=["consts"])
    C.ones8 = st.enter_context(nc.sbuf_tensor("ones8", [8, 1024], F32))
    S.I("pool", "memset", C.ones8[:], 1.0, writes=["ones8"])
    C.ident = st.enter_context(nc.sbuf_tensor("ident", [128, 128], BF16))
    C.cmask = st.enter_context(nc.sbuf_tensor("cmask", [128, 128], BF16))
    C.sel = st.enter_context(nc.sbuf_tensor("sel", [128, MH, 128], BF16))
    S.I("pool", "affine_select", out=C.ident[:], in_=C.ones[:], pattern=[[1, 128]], compare_op=ALU.is_equal, fill=0.0,
        base=0, channel_multiplier=-1, reads=["ones"], writes=["ident"])
    S.I("pool", "affine_select", out=C.cmask[:], in_=C.ones[:], pattern=[[1, 128]], compare_op=ALU.is_ge, fill=0.0,
        base=0, channel_multiplier=-1, reads=["ones"], writes=["cmask"])
    S.I("pool", "memset", C.cmask[0:64, 64:128], 0.0, reads=["cmask"], writes=["cmask"])
    S.I("pool", "memset", C.sel[:], 0.0, writes=["sel"])
    S.I("pool", "affine_select", out=C.sel[0:8, :, :], in_=C.ones8[:, 0:MH * 128].rearrange("p (h m) -> p h m", m=128),
        pattern=[[1, MH], [0, 128]], compare_op=ALU.is_equal, fill=0.0, base=0, channel_multiplier=-1,
        reads=["ones8", "sel"], writes=["sel"])


def mlstm_phase(C, src, dst, P, w_in, w_out):
    nc, S = C.nc, C.S
    from contextlib import ExitStack
    S.fence()
    TB = MTH // 128
    NCH = MTH // 64
    winv = w_in.rearrange("(kc p) f -> p kc f", p=128)
    OQ, OK_, OV, OO, OG = 0, 1024, 2048, 4096, 6144
    GROUPS = ([0, 1, 2], [3, 4, 5], [6, 7])
    with ExitStack() as st:
        def T(name, shape, dt):
            return st.enter_context(nc.sbuf_tensor(C.name(name), shape, dt))
        hT = T("mhT", [128, NDC, MTH], BF16)
        qT = T("mqT", [128, MH, MTH], BF16)
        kT = T("mkT", [128, MH, MTH], BF16)
        Vt = T("mVt", [128, TB, MH, 260], BF16)
        gT = T("mgT", [128, NDC, MTH], BF16)
        Cst = T("mCst", [128, MH, 260], F32)
        Cbf = [T("mCbf", [128, MH, 256], BF16) for _ in range(2)]
        Nb = [T("mNb", [128, MH, 128], BF16) for _ in range(2)]
        mcarry = T("mcarry", [8, 1], F32)
        negbf = T("negbf", [8, 1], F32)
        PT_ = [T("gP%d" % i, [8, MTH], F32) for i in range(5)]
        pk = ["gP%d" % i for i in range(5)]
        s_ = {n: T("s_" + n, [8, NCH], F32) for n in ["off", "cmax", "gneg", "gpos", "mnext", "mprev", "R", "dch", "offR"]}
        bsrc = {"e": 1, "dtok": 3, "E": 4}
        ghi = {n: T("ghi_" + n, [128, MTH], BF16) for n in bsrc}
        glo = {n: T("glo_" + n, [128, MTH], BF16) for n in bsrc}
        for n in bsrc:
            S.I("pool", "memset", ghi[n][:], 0.0, writes=["ghi_" + n])
            S.I("pool", "memset", glo[n][:], 0.0, writes=["glo_" + n])
        S.I("pool", "memset", Cst[:], 0.0, writes=[("Cst", h) for h in range(MH)])
        for i in range(2):
            S.I("pool", "memset", Cbf[i][:], 0.0, writes=[("Cbf", i, h) for h in range(MH)])
            S.I("pool", "memset", Nb[i][:], 0.0, writes=[("Nb", i, h) for h in range(MH)])
        S.I("pool", "memset", mcarry[:], 0.0, writes=["mcarry"])
        S.I("pool", "memset", s_["off"][:], 0.0, writes=["s_off"])
        S.I("pool", "memset", Vt[:, :, :, 256:257], 1.0, writes=[("Vt1",)])
        S.I("dve", "tensor_scalar", out=negbf[:], in0=P["bf"][0:8, :], scalar1=-1.0, scalar2=None, op0=ALU.mult,
            reads=["consts"], writes=["negbf"])
        G = lambda i: PT_[i][:]
        v3 = lambda i: PT_[i][:].rearrange("p (n l) -> p n l", l=64)
        bc3 = lambda t: t[:].unsqueeze(2).to_broadcast([8, NCH, 64])
        for ps_i in range(MLSTM_PASSES):
            with ExitStack() as st2:
                big = st2.enter_context(nc.sbuf_tensor(C.name("mbig"), [128, NDC, MTH], F32))
                rstd = st2.enter_context(nc.sbuf_tensor(C.name("mrstd"), [128, MTH], F32))
                tmp = st2.enter_context(nc.sbuf_tensor(C.name("mtmp"), [128, 512], F32))
                sqr = [st2.enter_context(nc.sbuf_tensor(C.name("msq"), [128, 512], BF16)) for _ in range(3)]
                wg = st2.enter_context(nc.sbuf_tensor(C.name("mwg"), [128, NDC, 16], F32))
                S.I("sp", "dma_start", out=wg[:], in_=winv[:, :, OG:OG + 16], writes=["mwg"], dma_key="mwg")
                load_norm_half(C, src, ps_i, big, hT, P["mix_norm"], rstd, tmp, sqr, th=MTH)
                for dc in range(NDC):
                    S.I("dve", "scalar_tensor_tensor", out=big[:, dc, :], in0=big[:, dc, :], scalar=P["mix_norm"][:, dc:dc + 1],
                        in1=rstd[:], op0=ALU.mult, op1=ALU.mult,
                        reads=[("acc", dc), ("rstd", 0), "consts", ("hT", dc, 0)], writes=[("acc", dc)])
                bi_, psi = C.bank()
                bf_, psf = C.bank()
                for (pp, b, c0) in ((psi, bi_, 0), (psf, bf_, 8)):
                    for kc in range(NDC):
                        S.I("pe", "matmul", pp[0:8, :], lhsT=wg[:, kc, c0:c0 + 8], rhs=big[:, kc, :],
                            start=(kc == 0), stop=(kc == NDC - 1), reads=["mwg", ("acc", kc)], writes=[("ps", b)])
                S.I("dve", "tensor_scalar", out=G(0), in0=psi[0:8, :], scalar1=P["bi"][0:8, :], scalar2=None, op0=ALU.add,
                    reads=[("ps", bi_), "consts"], writes=[pk[0]])
                S.I("act", "activation", out=G(1), in_=psf[0:8, :], func=AF.Exp, scale=-1.0, bias=negbf[:],
                    reads=[("ps", bf_), "negbf"], writes=[pk[1]])
                S.I("act", "activation", out=G(1), in_=G(1), func=AF.Ln, bias=C.one_col[0:8, :],
                    reads=[pk[1], "consts"], writes=[pk[1]])
                S.fence()
            S.I("dve", "tensor_tensor_scan", out=G(2), data0=C.ones8[:, 0:MTH], data1=G(1), initial=0.0,
                op0=ALU.mult, op1=ALU.add, reads=[pk[1], "ones8"], writes=[pk[2]])
            S.I("dve", "tensor_tensor", out=G(0), in0=G(0), in1=G(2), op=ALU.add, reads=[pk[0], pk[2]], writes=[pk[0]])
            S.I("dve", "tensor_copy", out=s_["off"][:, 1:NCH], in_=v3(2)[:, 0:NCH - 1, 63], reads=[pk[2]], writes=["s_off"])
            S.I("dve", "tensor_tensor", out=v3(0), in0=v3(0), in1=bc3(s_["off"]), op=ALU.subtract,
                reads=[pk[0], "s_off"], writes=[pk[0]])
            S.I("dve", "tensor_reduce", out=s_["cmax"][:], in_=v3(0), axis=mybir.AxisListType.X, op=ALU.max,
                reads=[pk[0]], writes=["s_cmax"])
            S.I("dve", "tensor_tensor", out=s_["gneg"][:], in0=v3(2)[:, :, 63], in1=s_["off"][:], op=ALU.subtract,
                reads=[pk[2], "s_off"], writes=["s_gneg"])
            S.I("dve", "tensor_scalar", out=s_["gpos"][:], in0=s_["gneg"][:], scalar1=-1.0, scalar2=None, op0=ALU.mult,
                reads=["s_gneg"], writes=["s_gpos"])
            S.I("dve", "tensor_tensor_scan", out=s_["mnext"][:], data0=s_["cmax"][:], data1=s_["gpos"][:], initial=mcarry[:],
                op0=ALU.max, op1=ALU.add, reads=["s_cmax", "s_gpos", "mcarry"], writes=["s_mnext"])
            S.I("dve", "tensor_copy", out=s_["mprev"][:, 0:1], in_=mcarry[:], reads=["mcarry"], writes=["s_mprev"])
            S.I("dve", "tensor_copy", out=s_["mprev"][:, 1:NCH], in_=s_["mnext"][:, 0:NCH - 1],
                reads=["s_mnext", "s_mprev"], writes=["s_mprev"])
            S.I("dve", "tensor_copy", out=mcarry[:], in_=s_["mnext"][:, NCH - 1:NCH], reads=["s_mnext", "s_mprev"], writes=["mcarry"])
            S.I("dve", "tensor_tensor", out=s_["R"][:], in0=s_["mprev"][:], in1=s_["cmax"][:], op=ALU.max,
                reads=["s_mprev", "s_cmax"], writes=["s_R"])
            S.I("dve", "tensor_tensor", out=v3(1), in0=v3(0), in1=bc3(s_["R"]), op=ALU.subtract,
                reads=[pk[0], "s_R", pk[1], pk[2]], writes=[pk[1]])
            S.I("act", "activation", out=G(1), in_=G(1), func=AF.Exp, reads=[pk[1]], writes=[pk[1]])
            S.I("dve", "tensor_tensor", out=s_["dch"][:], in0=s_["mprev"][:], in1=s_["R"][:], op=ALU.subtract,
                reads=["s_mprev", "s_R"], writes=["s_dch"])
            S.I("act", "activation", out=s_["dch"][:], in_=s_["dch"][:], func=AF.Exp, reads=["s_dch"], writes=["s_dch"])
            S.I("dve", "tensor_copy", out=v3(3), in_=bc3(s_["dch"]), reads=["s_dch"], writes=[pk[3]])
            S.I("dve", "tensor_tensor", out=s_["offR"][:], in0=s_["off"][:], in1=s_["R"][:], op=ALU.add,
                reads=["s_off", "s_R"], writes=["s_offR"])
            S.I("dve", "tensor_tensor", out=v3(4), in0=v3(2), in1=bc3(s_["offR"]), op=ALU.subtract,
                reads=[pk[2], "s_offR"], writes=[pk[4]])
            S.I("act", "activation", out=G(4), in_=G(4), func=AF.Exp, reads=[pk[4]], writes=[pk[4]])
            for n, pi in bsrc.items():
                S.I("act", "activation", out=ghi[n][0:8, :], in_=G(pi), func=AF.Copy, reads=[pk[pi]], writes=["ghi_" + n])
                S.I("dve", "tensor_tensor", out=glo[n][0:8, :], in0=G(pi), in1=ghi[n][0:8, :], op=ALU.subtract,
                    reads=[pk[pi], "ghi_" + n], writes=["glo_" + n])
            with ExitStack() as st2:
                wq = [st2.enter_context(nc.sbuf_tensor(C.name("mwq"), [128, NDC, 128], BF16)) for _ in range(3)]
                wv = [st2.enter_context(nc.sbuf_tensor(C.name("mwv"), [128, NDC, 512], BF16)) for _ in range(2)]
                cols = [OQ + h * 128 for h in range(MH)] + [OK_ + h * 128 for h in range(MH)]

                def ldq(i):
                    S.I("pool", "dma_start", out=wq[i % 3][:], in_=winv[:, :, cols[i]:cols[i] + 128],
                        writes=[("mwq", i % 3)], dma_key=("mwq", i % 3))

                def ldv(nt):
                    S.I("pool", "dma_start", out=wv[nt % 2][:], in_=winv[:, :, OV + nt * 512:OV + (nt + 1) * 512],
                        writes=[("mwv", nt % 2)], dma_key=("mwv", nt % 2))
                ldq(0)
                ldq(1)
                ldv(0)
                for i in range(2 * MH):
                    if i + 2 < 2 * MH:
                        ldq(i + 2)
                    b, ps = C.bank()
                    for kc in range(NDC):
                        S.I("pe", "matmul", ps[:], lhsT=wq[i % 3][:, kc, :], rhs=hT[:, kc, :], start=(kc == 0), stop=(kc == NDC - 1),
                            reads=[("mwq", i % 3), ("hT", kc, 0)], writes=[("ps", b)])
                    if i < MH:
                        S.I("act", "activation", out=qT[:, i, :], in_=ps[:], func=AF.Copy, scale=float(DK) ** -0.5,
                            reads=[("ps", b)], writes=[("qT", i)])
                    else:
                        S.I("act", "activation", out=kT[:, i - MH, :], in_=ps[:], func=AF.Copy,
                            reads=[("ps", b)], writes=[("kT", i - MH)])
                for nt in range(4):
                    if nt + 1 < 4:
                        ldv(nt + 1)
                    for tb in range(TB):
                        b, ps = C.bank()
                        for kc in range(NDC):
                            S.I("pe", "matmul", ps[:], lhsT=hT[:, kc, tb * 128:(tb + 1) * 128], rhs=wv[nt % 2][:, kc, :],
                                start=(kc == 0), stop=(kc == NDC - 1), reads=[("mwv", nt % 2), ("hT", kc, 0)], writes=[("ps", b)])
                        S.I("act", "activation", out=Vt[:, tb, 2 * nt:2 * nt + 2, 0:256],
                            in_=ps[:].rearrange("p (h d) -> p h d", d=256), func=AF.Copy,
                            reads=[("ps", b)], writes=[("Vt", tb, 2 * nt), ("Vt", tb, 2 * nt + 1)])
                S.fence()
            if 'nocore' in MLSTM_DBG:
                continue
            with ExitStack() as st2:
                def T2(name, shape, dt):
                    return st2.enter_context(nc.sbuf_tensor(C.name(name), shape, dt))
                kw = T2("kw", [128, MH, MTH], BF16)
                dq = T2("dq", [128, MH, MTH], BF16)
                dsc = T2("dsc", [128, MH, NCH], F32)
                Ebc = T2("Ebc", [128, MH, MTH], F32)
                hbuf = T2("hbuf", [128, MH, 2, MTH], F32)
                PTm = [T2("PTm", [128, 128], BF16) for _ in range(3)]
                kwt = [T2("kwt", [128, 128], BF16) for _ in range(3)]
                dnm = [T2("dnm", [128, 128], F32) for _ in range(3)]
                rcp = [T2("rcp", [128, 128], F32) for _ in range(3)]
                wog = [T2("wog", [128, NDC, 128], BF16) for _ in range(3)]
                og = [T2("og", [128, MTH], F32) for _ in range(2)]
                hsq = [T2("hsq", [128, MTH], BF16) for _ in range(2)]
                rrs = [T2("rrs", [128, MTH], F32) for _ in range(2)]

                def ldo(i):
                    S.I("pool", "dma_start", out=wog[i % 3][:], in_=winv[:, :, OO + i * 128:OO + (i + 1) * 128],
                        writes=[("wog", i % 3)], dma_key=("wog", i % 3))
                ldo(0)
                ldo(1)
                for h in range(MH):
                    bb = {}
                    for nm in ("e", "dtok", "E"):
                        b, pp = C.bank()
                        bb[nm] = (b, pp)
                        S.I("pe", "matmul", pp[:], lhsT=C.sel[:, h, :], rhs=ghi[nm][:], start=True, stop=False,
                            reads=["sel", "ghi_" + nm], writes=[("ps", b)])
                        S.I("pe", "matmul", pp[:], lhsT=C.sel[:, h, :], rhs=glo[nm][:], start=False, stop=True,
                            reads=["sel", "glo_" + nm], writes=[("ps", b)])
                    S.I("dve", "tensor_tensor", out=kw[:, h, :], in0=kT[:, h, :], in1=bb["e"][1][:], op=ALU.mult,
                        reads=[("kT", h), ("ps", bb["e"][0])], writes=[("kw", h)])
                    S.I("dve", "tensor_tensor", out=dq[:, h, :], in0=qT[:, h, :], in1=bb["dtok"][1][:], op=ALU.mult,
                        reads=[("qT", h), ("ps", bb["dtok"][0])], writes=[("dq", h)])
                    S.I("act", "activation", out=dsc[:, h, :], in_=bb["dtok"][1][:].rearrange("p (n l) -> p n l", l=64)[:, :, 0],
                        func=AF.Copy, reads=[("ps", bb["dtok"][0])], writes=[("dsc", h)])
                    S.I("act", "activation", out=Ebc[:, h, :], in_=bb["E"][1][:], func=AF.Copy,
                        reads=[("ps", bb["E"][0])], writes=[("Ebc", h)])
                pss, pst = C.banks[0], C.banks[1]
                pstb = pst[:].bitcast(BF16)
                for tb in range(TB):
                    bs = slice(tb * 128, (tb + 1) * 128)
                    for grp in GROUPS:
                        ih = list(enumerate(grp))
                        for i, h in ih:
                            S.I("pe", "matmul", pss[:, i * 128:(i + 1) * 128], lhsT=kw[:, h, bs], rhs=qT[:, h, bs],
                                start=True, stop=True, reads=[("kw", h), ("qT", h)], writes=[("ps", 0)])
                        for i, h in ih:
                            S.I("dve", "tensor_tensor", out=PTm[i][:], in0=pss[:, i * 128:(i + 1) * 128], in1=C.cmask[:],
                                op=ALU.mult, reads=[("ps", 0), "cmask"], writes=[("PTm", i)])
                        for i, h in ih:
                            S.I("pe", "transpose", pstb[:, i * 128:(i + 1) * 128], kw[:, h, bs], C.ident[:],
                                reads=[("kw", h), "ident"], writes=[("ps", 1)])
                        for i, h in ih:
                            S.I("act", "activation", out=kwt[i][:], in_=pstb[:, i * 128:(i + 1) * 128], func=AF.Copy,
                                reads=[("ps", 1)], writes=[("kwt", i)])

                        def upd(ci, dst_i):
                            cs = ci * 64
                            n = 2 * tb + ci
                            for i, h in ih:
                                psc = C.banks[2 + i]
                                S.I("pe", "matmul", psc[:, 0:257], lhsT=kwt[i][cs:cs + 64, :], rhs=Vt[cs:cs + 64, tb, h, 0:257],
                                    start=True, stop=True, reads=[("kwt", i), ("Vt", tb, h), ("Vt1",)], writes=[("ps", 2 + i)])
                            for i, h in ih:
                                psc = C.banks[2 + i]
                                S.I("dve", "scalar_tensor_tensor", out=Cst[:, h, 0:257], in0=Cst[:, h, 0:257],
                                    scalar=dsc[:, h, n:n + 1], in1=psc[:, 0:257], op0=ALU.mult, op1=ALU.add,
                                    reads=[("Cst", h), ("dsc", h), ("ps", 2 + i)], writes=[("Cst", h)])
                            for i, h in ih:
                                S.I("act", "activation", out=Cbf[dst_i][:, h, :], in_=Cst[:, h, 0:256], func=AF.Copy,
                                    reads=[("Cst", h)], writes=[("Cbf", dst_i, h)])
                            for i, h in ih:
                                S.I("dve", "tensor_scalar", out=Nb[dst_i][:, h, :], in0=C.ones[:], scalar1=Cst[:, h, 256:257],
                                    scalar2=None, op0=ALU.mult, reads=[("Cst", h), "ones"], writes=[("Nb", dst_i, h)])
                        upd(0, 1)
                        xloc = {}
                        for i, h in ih:
                            for part, (lw0, lw1, lwi) in enumerate((
                                    (Cbf[0][:, h, 0:128], Cbf[1][:, h, 0:128], Vt[:, tb, h, 0:128]),
                                    (Cbf[0][:, h, 128:256], Cbf[1][:, h, 128:256], Vt[:, tb, h, 128:256]),
                                    (Nb[0][:, h, :], Nb[1][:, h, :], C.ones[:]))):
                                gi = i * 3 + part
                                bx = 5 + gi // 4
                                c0 = (gi % 4) * 128
                                psx = C.banks[bx]
                                xloc[(i, part)] = (bx, c0)
                                rd = [("Vt", tb, h), ("PTm", i), ("Cbf", 0, h), ("Cbf", 1, h), ("Nb", 0, h), ("Nb", 1, h),
                                      ("dq", h), "ones"]
                                wk_ = [("ps", bx)]
                                S.I("pe", "matmul", psx[:, c0:c0 + 128], lhsT=lwi, rhs=PTm[i][:], start=True, stop=False,
                                    reads=rd, writes=wk_)
                                S.I("pe", "matmul", psx[:, c0:c0 + 64], lhsT=lw0, rhs=dq[:, h, tb * 128:tb * 128 + 64],
                                    start=False, stop=False, reads=rd, writes=wk_)
                                S.I("pe", "matmul", psx[:, c0 + 64:c0 + 128], lhsT=lw1, rhs=dq[:, h, tb * 128 + 64:tb * 128 + 128],
                                    start=False, stop=True, reads=rd, writes=wk_)
                        upd(1, 0)
                        for i, h in ih:
                            bx, c0 = xloc[(i, 2)]
                            S.I("act", "activation", out=dnm[i][:], in_=C.banks[bx][:, c0:c0 + 128], func=AF.Abs,
                                reads=[("ps", bx)], writes=[("dnm", i)])
                        for i, h in ih:
                            S.I("dve", "tensor_tensor", out=dnm[i][:], in0=dnm[i][:], in1=Ebc[:, h, bs], op=ALU.max,
                                reads=[("dnm", i), ("Ebc", h)], writes=[("dnm", i)])
                        for i, h in ih:
                            S.I("dve", "reciprocal", out=rcp[i][:], in_=dnm[i][:], reads=[("dnm", i)], writes=[("rcp", i)])
                        for dvh in range(2):
                            for i, h in ih:
                                bx, c0 = xloc[(i, dvh)]
                                S.I("dve", "tensor_tensor", out=hbuf[:, h, dvh, bs], in0=C.banks[bx][:, c0:c0 + 128],
                                    in1=rcp[i][:], op=ALU.mult, reads=[("ps", bx), ("rcp", i)],
                                    writes=[("hbuf", h, dvh)])
                S.fence()
                for h in range(MH):
                    r = h % 2
                    b7, ps7 = C.bank()
                    for dvh in range(2):
                        S.I("act", "activation", out=hsq[dvh][:], in_=hbuf[:, h, dvh, :], func=AF.Square,
                            reads=[("hbuf", h, dvh)], writes=[("hsq", dvh)])
                        S.I("pe", "matmul", ps7[:], lhsT=C.ones[:], rhs=hsq[dvh][:], start=(dvh == 0), stop=(dvh == 1),
                            reads=[("hsq", dvh), "ones"], writes=[("ps", b7)])
                    S.I("act", "activation", out=rrs[r][:], in_=ps7[:], func=AF.Ln, scale=1.0 / DV, bias=C.eps_col[:],
                        reads=[("ps", b7), "consts"], writes=[("rrs", r)])
                    S.I("act", "activation", out=rrs[r][:], in_=rrs[r][:], func=AF.Exp, scale=-0.5, reads=[("rrs", r)], writes=[("rrs", r)])
                    for dvh in range(2):
                        i = 2 * h + dvh
                        if i + 2 < NDC:
                            ldo(i + 2)
                        b8, ps8 = C.bank()
                        for kc in range(NDC):
                            S.I("pe", "matmul", ps8[:], lhsT=wog[i % 3][:, kc, :], rhs=hT[:, kc, :],
                                start=(kc == 0), stop=(kc == NDC - 1), reads=[("wog", i % 3), ("hT", kc, 0)], writes=[("ps", b8)])
                        S.I("act", "activation", out=og[dvh][:], in_=ps8[:], func=AF.Sigmoid, bias=P["bo"][:, i:i + 1],
                            reads=[("ps", b8), "consts"], writes=[("og", dvh)])
                        S.I("dve", "scalar_tensor_tensor", out=hbuf[:, h, dvh, :], in0=hbuf[:, h, dvh, :],
                            scalar=P["hnorm"][:, i:i + 1], in1=rrs[r][:], op0=ALU.mult, op1=ALU.mult,
                            reads=[("hbuf", h, dvh), ("rrs", r), "consts"], writes=[("hbuf", h, dvh)])
                        S.I("dve", "tensor_tensor", out=gT[:, i, :], in0=hbuf[:, h, dvh, :], in1=og[dvh][:], op=ALU.mult,
                            reads=[("hbuf", h, dvh), ("og", dvh)], writes=[("gT", i)])
                S.fence()
            out_proj_residual(C, src, dst, ps_i, gT, lambda kc, tt: ("gT", kc), w_out, None, th=MTH)
    S.fence()


LAYER_KINDS = ("attn", "conv", "mlstm", "attn")


def _col16(v):
    return np.ascontiguousarray(np.asarray(v, np.float32).reshape(16, 128).T)


def _pad8(v):
    out = np.zeros((128, 1), np.float32)
    out[:8, 0] = np.asarray(v, np.float32)
    return out


def pack_params(inp):
    cols, offs, o = [], {}, 0

    def add(name, a):
        nonlocal o
        a = np.asarray(a, np.float32)
        offs[name] = (o, a.shape[1])
        o += a.shape[1]
        cols.append(a)
    for i, kind in enumerate(LAYER_KINDS):
        p = "l%d_" % i
        add(p + "mix_norm", _col16(inp[p + "mix_norm"]))
        add(p + "mlp_norm", _col16(inp[p + "mlp_norm"]))
        if kind == "attn":
            add(p + "gq", np.tile(np.asarray(inp[p + "attn_q_norm"], np.float32), 2)[:, None])
            add(p + "gk", np.tile(np.asarray(inp[p + "attn_k_norm"], np.float32), 2)[:, None])
            add(p + "sinks", np.ascontiguousarray(np.repeat(np.asarray(inp[p + "attn_sinks"], np.float32).reshape(16, 2), 64, axis=1).T))
        elif kind == "conv":
            b_in = np.asarray(inp[p + "conv_b_in"], np.float32)
            add(p + "b_in_a", _col16(b_in[:D]))
            add(p + "b_in_g", _col16(b_in[D:]))
            dw = np.asarray(inp[p + "conv_dw"], np.float32)
            add(p + "dw", np.ascontiguousarray(dw.T.reshape(16, 128, CONVW).transpose(1, 0, 2).reshape(128, 16 * CONVW)))
            add(p + "dw_b", _col16(inp[p + "conv_dw_b"]))
            add(p + "ln_g", _col16(inp[p + "conv_ln_g"]))
            add(p + "ln_b", _col16(inp[p + "conv_ln_b"]))
            add(p + "b_out", _col16(inp[p + "conv_b_out"]))
        else:
            bg = np.asarray(inp[p + "mlstm_b_gates"], np.float32)
            add(p + "bo", _col16(bg[:D]))
            add(p + "bi", _pad8(bg[D:D + 8]))
            add(p + "bf", _pad8(bg[D + 8:D + 16]))
            add(p + "hnorm", _col16(inp[p + "mlstm_h_norm"]))
    return np.ascontiguousarray(np.concatenate(cols, axis=1)), offs


WSHAPES = {
    "attn_w_qkv": [D, 2560], "attn_w_o": [D, D], "conv_w_in": [D, 2 * D], "conv_w_out": [D, D],
    "mlstm_w_in": [D, 6160], "mlstm_w_out": [D, D], "mlp_w1": [D, DFF], "mlp_w2": [DFF, D],
}
LAYER_W = {"attn": ("attn_w_qkv", "attn_w_o"), "conv": ("conv_w_in", "conv_w_out"), "mlstm": ("mlstm_w_in", "mlstm_w_out")}


def build_program(offs, NP, layers=(0, 1, 2, 3)):
    from contextlib import ExitStack
    nc = bass.Bass("TRN2", target_bir_lowering=False)
    xT = nc.dram_tensor("xT", [D, SEQ], F32, kind="ExternalInput").ap()
    prm = nc.dram_tensor("prm", [128, NP], F32, kind="ExternalInput").ap()
    W = {}
    for i in layers:
        kind = LAYER_KINDS[i]
        for wn in LAYER_W[kind] + ("mlp_w1", "mlp_w2"):
            nm = "l%d_%s" % (i, wn)
            W[nm] = nc.dram_tensor(nm, WSHAPES[wn], F32, kind="ExternalInput").ap()
    yT = nc.dram_tensor("yT", [D, SEQ], F32, kind="ExternalOutput").ap()
    xs = nc.dram_tensor("xs_scratch", [D, SEQ], F32, kind="Internal").ap()
    S = Sched(nc)
    with ExitStack() as st:
        C = Ctx(nc, S, st)
        attn_consts(C)
        mlstm_consts(C)
        pt = st.enter_context(nc.sbuf_tensor("prm_sb", [128, NP], F32))
        S.I("sp", "dma_start", out=pt[:], in_=prm[:, :], writes=["consts"], dma_key="prm")
        cur = xT
        for n, i in enumerate(layers):
            kind = LAYER_KINDS[i]
            p = "l%d_" % i
            P = {k[len(p):]: pt[:, o:o + w] for k, (o, w) in offs.items() if k.startswith(p)}
            wa, wb = LAYER_W[kind]
            if kind == "attn":
                attn_phase(C, cur, xs, P, W[p + wa], W[p + wb])
            elif kind == "conv":
                conv_phase(C, cur, xs, P, W[p + wa], W[p + wb])
            else:
                mlstm_phase(C, cur, xs, P, W[p + wa], W[p + wb])
            last = (n == len(layers) - 1)
            S.fence()
            mlp_phase(C, xs, yT if last else xs, P["mlp_norm"], W[p + "mlp_w1"], W[p + "mlp_w2"])
            S.fence()
            cur = xs
        S.emit()
    return nc, S


def kernel(**inputs):
    x = np.asarray(inputs["x"], np.float32)
    B = x.shape[0]
    prm, offs = pack_params(inputs)
    nc, _ = build_program(offs, prm.shape[1])
    shared = {"prm": prm}
    for i, kind in enumerate(LAYER_KINDS):
        for wn in LAYER_W[kind] + ("mlp_w1", "mlp_w2"):
            nm = "l%d_%s" % (i, wn)
            shared[nm] = np.ascontiguousarray(np.asarray(inputs[nm], np.float32))
    in_maps = []
    for b in range(B):
        m = dict(shared)
        m["xT"] = np.ascontiguousarray(x[b].T)
        in_maps.append(m)
    res = run_bass_kernel_spmd(nc, in_maps, core_ids=list(range(B)))
    out = np.empty_like(x)
    for b in range(B):
        out[b] = np.asarray(res.results[b]["yT"]).T
    return out
```
